# Optimizing a Trainium2 kernel written in Bass

```python
import math
import jax, jax.numpy as jnp
from jax import lax
import numpy as np

D_MODEL = 2048
BATCH = 4
SEQ = 8192
DEPTH = 1
DEC_BATCH = 16
DEC_SEQ = 2048
PAST_LEN = 128

HEAD_DIM = 128
ATTN_Q_HEADS = 8
ATTN_KV_HEADS = 2
ATTN_GROUP = ATTN_Q_HEADS // ATTN_KV_HEADS
ATTN_WIDTH = ATTN_Q_HEADS * HEAD_DIM
ATTN_KV_WIDTH = ATTN_KV_HEADS * HEAD_DIM
RET_HEADS = 8
RET_WIDTH = RET_HEADS * HEAD_DIM
MIX_WIDTH = ATTN_WIDTH + RET_WIDTH
IN_WIDTH = ATTN_WIDTH + 2 * ATTN_KV_WIDTH + 4 * RET_WIDTH
WINDOW = 128
RET_CHUNK = 128
FFN_HIDDEN = ((8 * D_MODEL + 3 * 256 - 1) // (3 * 256)) * 256
DEEPNORM_ALPHA = (2.0 * DEPTH) ** 0.25
DEEPNORM_BETA = (8.0 * DEPTH) ** -0.25
LN_EPS = 1e-5
GN_EPS = 1e-5
NEG_INF = -1e30

kernel_name = "hybrid_bidir_swa_retention_encoder"


def _layer_norm(x, gain, bias):
    xf = x.astype(jnp.float32)
    mu = jnp.mean(xf, axis=-1, keepdims=True)
    var = jnp.mean(jnp.square(xf - mu), axis=-1, keepdims=True)
    y = (xf - mu) * lax.rsqrt(var + LN_EPS) * gain.astype(jnp.float32) + bias.astype(jnp.float32)
    return y.astype(x.dtype)


def _windowed_gqa_alibi_sink(q, k, v, sink):
    B, S, _ = q.shape
    W = WINDOW
    N = S // W
    qb = q.reshape(B, N, W, ATTN_KV_HEADS, ATTN_GROUP, HEAD_DIM)
    k = k.reshape(B, S, ATTN_KV_HEADS, HEAD_DIM)
    v = v.reshape(B, S, ATTN_KV_HEADS, HEAD_DIM)
    pad = ((0, 0), (W, W), (0, 0), (0, 0))
    kp = jnp.pad(k, pad).reshape(B, N + 2, W, ATTN_KV_HEADS, HEAD_DIM)
    vp = jnp.pad(v, pad).reshape(B, N + 2, W, ATTN_KV_HEADS, HEAD_DIM)
    kb = jnp.concatenate([kp[:, :-2], kp[:, 1:-1], kp[:, 2:]], axis=2)
    vb = jnp.concatenate([vp[:, :-2], vp[:, 1:-1], vp[:, 2:]], axis=2)
    s = jnp.einsum('bnqkgd,bnskd->bnkgqs', qb, kb).astype(jnp.float32) * (HEAD_DIM ** -0.5)
    qpos = jnp.arange(W)
    kpos = jnp.arange(3 * W) - W
    dist = jnp.abs(qpos[:, None] - kpos[None, :])
    key_global = jnp.arange(N)[:, None] * W + kpos[None, :]
    valid = (dist <= W)[None] & ((key_global >= 0) & (key_global < S))[:, None, :]
    slopes = (2.0 ** (-8.0 * jnp.arange(1, ATTN_Q_HEADS + 1, dtype=jnp.float32) / ATTN_Q_HEADS)
              ).reshape(ATTN_KV_HEADS, ATTN_GROUP)
    s = s - slopes[:, :, None, None] * dist.astype(jnp.float32)
    s = jnp.where(valid[None, :, None, None], s, NEG_INF)
    sink_l = sink.astype(jnp.float32).reshape(ATTN_KV_HEADS, ATTN_GROUP)[:, :, None, None]
    m = jnp.maximum(jnp.max(s, axis=-1, keepdims=True), sink_l)
    e = jnp.exp(s - m)
    p = e / (jnp.sum(e, axis=-1, keepdims=True) + jnp.exp(sink_l - m))
    o = jnp.einsum('bnkgqs,bnskd->bnqkgd', p.astype(v.dtype), vb)
    return o.reshape(B, S, ATTN_WIDTH)


def _retention_direction(q, k, v, log_gamma, include_diag):
    B, S, H, Dk = q.shape
    Dv = v.shape[-1]
    C = RET_CHUNK
    N = S // C
    qc = q.reshape(B, N, C, H, Dk)
    kc = k.reshape(B, N, C, H, Dk)
    vc = v.reshape(B, N, C, H, Dv)
    pos = jnp.arange(C, dtype=jnp.float32)
    diff = pos[:, None] - pos[None, :]
    keep = (diff >= 0) if include_diag else (diff > 0)
    decay_intra = jnp.where(keep[None], jnp.exp(log_gamma[:, None, None] * jnp.maximum(diff, 0.0)[None]), 0.0)
    scores = jnp.einsum('bnihd,bnjhd->bnhij', qc, kc) * decay_intra
    intra = jnp.einsum('bnhij,bnjhe->bnihe', scores, vc)
    k_decay = jnp.exp(log_gamma[None, :] * (C - 1.0 - pos)[:, None])
    chunk_kv = jnp.einsum('bnjhd,bnjhe->nbhde', kc * k_decay[:, :, None], vc)
    chunk_decay = jnp.exp(log_gamma * C)[:, None, None]

    def step(state, kv):
        return state * chunk_decay + kv, state

    _, past = lax.scan(step, jnp.zeros((B, H, Dk, Dv), chunk_kv.dtype), chunk_kv)
    q_decay = jnp.exp(log_gamma[None, :] * (pos + 1.0)[:, None])
    cross = jnp.einsum('bnihd,nbhde->bnihe', qc * q_decay[:, :, None], past)
    return (intra + cross).reshape(B, S, H, Dv)


def _bidirectional_retention(q, k, v, g, decay_fwd, decay_bwd, gn_gain):
    B, S, _ = q.shape
    q = q.reshape(B, S, RET_HEADS, HEAD_DIM)
    k = k.reshape(B, S, RET_HEADS, HEAD_DIM) * (HEAD_DIM ** -0.5)
    v = v.reshape(B, S, RET_HEADS, HEAD_DIM)
    lg_f = jax.nn.log_sigmoid(decay_fwd.astype(jnp.float32))
    lg_b = jax.nn.log_sigmoid(decay_bwd.astype(jnp.float32))
    fwd = _retention_direction(q, k, v, lg_f, True)
    bwd = jnp.flip(_retention_direction(jnp.flip(q, 1), jnp.flip(k, 1), jnp.flip(v, 1), lg_b, False), 1)
    o = (fwd + bwd).astype(jnp.float32)
    mu = jnp.mean(o, axis=-1, keepdims=True)
    var = jnp.mean(jnp.square(o - mu), axis=-1, keepdims=True)
    o = ((o - mu) * lax.rsqrt(var + GN_EPS)).reshape(B, S, RET_WIDTH) * gn_gain.astype(jnp.float32)
    return (jax.nn.silu(g.astype(jnp.float32)) * o).astype(g.dtype)


def _trunk(x, w_in, attn_sink, ret_decay_fwd, ret_decay_bwd, ret_gn_gain, w_out,
           ln1_gain, ln1_bias, w_ffn_in, w_ffn_out, ln2_gain, ln2_bias):
    for l in range(DEPTH):
        proj = x @ w_in[l]
        o = 0
        q_a = proj[..., o:o + ATTN_WIDTH]; o += ATTN_WIDTH
        k_a = proj[..., o:o + ATTN_KV_WIDTH]; o += ATTN_KV_WIDTH
        v_a = proj[..., o:o + ATTN_KV_WIDTH]; o += ATTN_KV_WIDTH
        q_r = proj[..., o:o + RET_WIDTH]; o += RET_WIDTH
        k_r = proj[..., o:o + RET_WIDTH]; o += RET_WIDTH
        v_r = proj[..., o:o + RET_WIDTH]; o += RET_WIDTH
        g_r = proj[..., o:o + RET_WIDTH]
        attn = _windowed_gqa_alibi_sink(q_a, k_a, v_a, attn_sink[l])
        ret = _bidirectional_retention(q_r, k_r, v_r, g_r, ret_decay_fwd[l], ret_decay_bwd[l], ret_gn_gain[l])
        mix = jnp.concatenate([attn.astype(x.dtype), ret.astype(x.dtype)], axis=-1) @ w_out[l]
        x = _layer_norm(DEEPNORM_ALPHA * x + mix, ln1_gain[l], ln1_bias[l])
        gu = x @ w_ffn_in[l]
        ffn = (jax.nn.silu(gu[..., :FFN_HIDDEN]) * gu[..., FFN_HIDDEN:]) @ w_ffn_out[l]
        x = _layer_norm(DEEPNORM_ALPHA * x + ffn, ln2_gain[l], ln2_bias[l])
    return x


def setup_inputs(seed: int = 0) -> dict:
    key = jax.random.key(seed)
    ks = jax.random.split(key, 16)
    f32 = jnp.float32
    x_prompt = jax.random.normal(ks[0], (BATCH, SEQ, D_MODEL), f32)
    x_sample = jax.random.normal(ks[1], (DEC_BATCH, DEC_SEQ, D_MODEL), f32)
    col_scale = jnp.concatenate([
        jnp.ones((ATTN_WIDTH + ATTN_KV_WIDTH,), f32),
        jnp.full((ATTN_KV_WIDTH,), DEEPNORM_BETA, f32),
        jnp.ones((2 * RET_WIDTH,), f32),
        jnp.full((RET_WIDTH,), DEEPNORM_BETA, f32),
        jnp.ones((RET_WIDTH,), f32)])
    w_in = jax.random.normal(ks[2], (DEPTH, D_MODEL, IN_WIDTH), f32) * (D_MODEL ** -0.5) * col_scale
    attn_sink = 0.5 * jax.random.normal(ks[3], (DEPTH, ATTN_Q_HEADS), f32)
    base = jnp.log(2.0 ** (5.0 + jnp.arange(RET_HEADS, dtype=f32)) - 1.0)
    ret_decay_fwd = base[None] + 0.1 * jax.random.normal(ks[4], (DEPTH, RET_HEADS), f32)
    ret_decay_bwd = base[None] + 0.1 * jax.random.normal(ks[5], (DEPTH, RET_HEADS), f32)
    ret_gn_gain = 1.0 + 0.02 * jax.random.normal(ks[6], (DEPTH, RET_WIDTH), f32)
    w_out = jax.random.normal(ks[7], (DEPTH, MIX_WIDTH, D_MODEL), f32) * (MIX_WIDTH ** -0.5) * DEEPNORM_BETA
    ln1_gain = 1.0 + 0.02 * jax.random.normal(ks[8], (DEPTH, D_MODEL), f32)
    ln1_bias = 0.01 * jax.random.normal(ks[9], (DEPTH, D_MODEL), f32)
    w_ffn_in = jax.random.normal(ks[10], (DEPTH, D_MODEL, 2 * FFN_HIDDEN), f32) * (D_MODEL ** -0.5) * DEEPNORM_BETA
    w_ffn_out = jax.random.normal(ks[11], (DEPTH, FFN_HIDDEN, D_MODEL), f32) * (FFN_HIDDEN ** -0.5) * DEEPNORM_BETA
    ln2_gain = 1.0 + 0.02 * jax.random.normal(ks[12], (DEPTH, D_MODEL), f32)
    ln2_bias = 0.01 * jax.random.normal(ks[13], (DEPTH, D_MODEL), f32)
    return {"x_prompt": x_prompt, "x_sample": x_sample, "w_in": w_in, "attn_sink": attn_sink,
            "ret_decay_fwd": ret_decay_fwd, "ret_decay_bwd": ret_decay_bwd, "ret_gn_gain": ret_gn_gain,
            "w_out": w_out, "ln1_gain": ln1_gain, "ln1_bias": ln1_bias, "w_ffn_in": w_ffn_in,
            "w_ffn_out": w_ffn_out, "ln2_gain": ln2_gain, "ln2_bias": ln2_bias}


def reference(x_prompt, x_sample, w_in, attn_sink, ret_decay_fwd, ret_decay_bwd, ret_gn_gain, w_out,
              ln1_gain, ln1_bias, w_ffn_in, w_ffn_out, ln2_gain, ln2_bias):
    y_prompt = _trunk(x_prompt, w_in, attn_sink, ret_decay_fwd, ret_decay_bwd, ret_gn_gain, w_out,
                      ln1_gain, ln1_bias, w_ffn_in, w_ffn_out, ln2_gain, ln2_bias)
    y_sample = _trunk(x_sample, w_in, attn_sink, ret_decay_fwd, ret_decay_bwd, ret_gn_gain, w_out,
                      ln1_gain, ln1_bias, w_ffn_in, w_ffn_out, ln2_gain, ln2_bias)
    return (y_prompt, y_sample)
```

```python
import contextlib
import os
import numpy as np
import concourse.bass as bass
import concourse.mybir as mybir
from concourse.bass_utils import run_bass_kernel_spmd

F32 = mybir.dt.float32
BF16 = mybir.dt.bfloat16
AF = mybir.ActivationFunctionType
ALU = mybir.AluOpType

P = 128
D = 2048
KC = 16
T = 512
NB = 4
IN_W = 5632
FH = 5632
FCH = 44
ALPHA = 2.0 ** 0.25
EPS = 1e-5
SCALE = 128.0 ** -0.5
LNS = float(np.log(SCALE))
NEG = -30000.0

C_RP, C_RN, C_IP1, C_CMI, C_C127, C_CJ, C_BIAS, C_ID, C_END = 0, 128, 256, 384, 512, 513, 514, 514 + 3072, 514 + 3072 + 128
TB_LGF, TB_LGB, TB_GCF, TB_GCB, TB_ES, TB_KDF, TB_KDB, TB_LINK, TB_LB = 0, 8, 16, 24, 32, 40, 48, 56, 57
G_WOUT, G_FIN, G_FOUT, G_TOT = 22, 30, 74, 106


class Op:
    __slots__ = ("eng", "fn", "deps", "chan", "sig", "val")

    def __init__(self, eng, fn, deps, chan):
        self.eng = eng
        self.fn = fn
        self.deps = deps
        self.chan = chan
        self.sig = chan is not None
        self.val = 0


class Prog:
    def __init__(self):
        self.ops = []
        self.lw = {}
        self.rd = {}

    def add(self, eng, fn, reads=(), writes=(), chan=None):
        idx = len(self.ops)
        ops = self.ops
        deps = set()
        lw = self.lw
        rd = self.rd
        for k in reads:
            w = lw.get(k)
            if w is not None:
                deps.add(w)
        for k in writes:
            w = lw.get(k)
            if w is not None:
                deps.add(w)
            r = rd.get(k)
            if r:
                deps.update(r.values())
        rkey = chan if chan is not None else eng
        for k in reads:
            r = rd.get(k)
            if r is None:
                rd[k] = {rkey: idx}
            else:
                r[rkey] = idx
        for k in writes:
            lw[k] = idx
            rd[k] = None
        if eng == "pe":
            deps = [d for d in deps if not (ops[d].eng == "pe" and ops[d].chan is None)]
        else:
            deps = list(deps)
        for d in deps:
            ops[d].sig = True
        ops.append(Op(eng, fn, deps, chan))
        return idx

    def emit(self, nc, es, final_chans):
        engs = ["pe", "dve", "act", "pool", "sp"]
        sems = {e: es.enter_context(nc.semaphore("s_" + e)) for e in engs}
        chans = {}
        cnt = {}
        for op in self.ops:
            if op.chan is not None:
                if op.chan not in chans:
                    chans[op.chan] = es.enter_context(nc.semaphore("c_" + str(op.chan)))
                    cnt[op.chan] = 0
                cnt[op.chan] += 16
                op.val = cnt[op.chan]
            elif op.sig:
                cnt[op.eng] = cnt.get(op.eng, 0) + 1
                op.val = cnt[op.eng]
        per = {e: [] for e in engs}
        for op in self.ops:
            per[op.eng].append(op)
        ops = self.ops

        def run(e, eobj):
            known = {}
            for op in per[e]:
                waits = {}
                for d in op.deps:
                    p = ops[d]
                    s = chans[p.chan] if p.chan is not None else sems[p.eng]
                    key = id(s)
                    if key not in waits or waits[key][1] < p.val:
                        waits[key] = (s, p.val)
                for key, (s, v) in waits.items():
                    if known.get(key, 0) >= v:
                        continue
                    eobj.wait_ge(s, v)
                    known[key] = v
                ins = op.fn(eobj)
                if op.chan is not None:
                    ins.then_inc(chans[op.chan], 16)
                elif op.sig:
                    ins.then_inc(sems[e], 1)
            if e == "sp":
                for c in final_chans:
                    if c in chans:
                        eobj.wait_ge(chans[c], cnt[c])

        block = es.enter_context(nc.Block())
        block.tensor(lambda t: run("pe", t))
        block.vector(lambda v: run("dve", v))
        block.scalar(lambda a: run("act", a))
        block.gpsimd(lambda g: run("pool", g))
        block.sync(lambda s: run("sp", s))


def build(NT=16, SEGB=16, STAGE=9):
    NBLK = NT * NB
    NTOK = NBLK * P
    nc = bass.Bass("TRN2", target_bir_lowering=False)
    es = contextlib.ExitStack()
    pg = Prog()

    def din(name, shape):
        return nc.dram_tensor(name, shape, F32, kind="ExternalInput").ap()

    x_d = din("x", [NTOK, D])
    w_in_d = din("w_in", [D, IN_W])
    w_out_d = din("w_out", [D, D])
    w_fi_d = din("w_ffn_in", [D, 2 * FH])
    w_fo_d = din("w_ffn_out", [FH, D])
    cst_d = din("cst", [P, C_END])
    prm_d = din("prm", [P, 32])
    gnb_d = din("gnb", [P, 1024])
    lnb_d = din("lnb", [P, 4 * D])
    y_d = nc.dram_tensor("y", [NTOK, D], F32, kind="ExternalOutput").ap()
    Ws = nc.dram_tensor("Ws", [G_TOT, P, 4096], BF16).ap()
    RetRec = nc.dram_tensor("RetRec", [NBLK, P, 4096], BF16).ap()
    AttRec = nc.dram_tensor("AttRec", [NBLK, P, 512], BF16).ap()

    def sb(name, shape, dt):
        return es.enter_context(nc.sbuf_tensor("sb_" + name, shape, dt))

    ring = [sb("ring%d" % i, [P, 4096], BF16) for i in range(3)]
    xbf = [sb("xbf%d" % i, [P, D], BF16) for i in range(2)]
    xT = sb("xT", [P, KC, T], BF16)
    x1buf = sb("x1buf", [P, NB, D], F32)
    arena = sb("arena", [P, FCH * T], BF16)
    attrec = sb("attrec", [P, 4, 512], BF16)
    retrec = [sb("retrec%d" % i, [P, 4, 1024], BF16) for i in range(2)]
    S = sb("S", [P, 1024], F32)
    Sfb = sb("Sfb", [P, 1024], BF16)
    ident = sb("ident", [P, P], BF16)
    ones = sb("ones", [P, P], BF16)
    biasT = sb("biasT", [P, 3, 8, P], BF16)
    Dcomb = sb("Dcomb", [P, 8, P], BF16)
    AFt = sb("AFt", [P, 8, P], BF16)
    ABt = sb("ABt", [P, 8, P], BF16)
    gnb = sb("gnb", [P, 1024], F32)
    lnb = sb("lnb", [P, 4, D], F32)
    tab = sb("tab", [P, 64], F32)
    prm = sb("prm", [P, 32], F32)
    esr = sb("esr", [P, 8, P], BF16)
    lbrow = sb("lbrow", [P, P], BF16)
    ones512 = sb("ones512", [1, 512], BF16)
    xb1 = xbf[0]
    ftmp = [sb("ftmp%d" % i, [P, 512], F32) for i in range(2)]
    st6 = sb("st6", [P, 8, 6], F32)
    mv = sb("mv", [P, 8, 2], F32)
    sd = sb("sd", [P, 8], F32)
    rstd = sb("rstd", [P, 8], F32)
    nb = sb("nb", [P, 8], F32)
    psb = [es.enter_context(nc.psum_tensor("ps%d" % i, [P, 512], F32)) for i in range(8)]

    hT = arena[:].rearrange("p (c t) -> p c t", c=FCH, t=T)
    QaT = arena[:, 0:8 * T].rearrange("p (c t) -> p c t", c=8, t=T)
    QrT = arena[:, 8 * T:16 * T].rearrange("p (c t) -> p c t", c=8, t=T)
    gsg = arena[:, 16 * T:24 * T].rearrange("p (b c) -> p b c", b=NB, c=1024)
    mixT = arena[:, 24 * T:40 * T].rearrange("p (c t) -> p c t", c=KC, t=T)
    krb = retrec[0]
    kdb = retrec[1]
    cvb = arena[:, 16384:20480]
    x1b16 = x1buf[:].rearrange("p b d -> p (b d)").bitcast(BF16)
    x1f = x1buf[:].rearrange("p b d -> p (b d)")
    rec1 = arena[:, 0:16384].rearrange("p (b q c) -> p b q c", b=NB, q=4, c=1024)
    stage = [x1f[:, 0:4096], x1f[:, 4096:8192]]
    def scr32(b, off, n):
        return x1f[:, b * D + off: b * D + off + n]

    def scr16(b, off32, n):
        return x1b16[:, (b * D + off32) * 2: (b * D + off32) * 2 + n]

    expT = [scr32(0, 0, 512), scr32(0, 512, 512)]
    PT = scr16(0, 1024, 1536).rearrange("p (k c) -> p k c", k=3, c=512)
    rden = scr32(1, 0, 512)
    PrT = scr16(1, 512, 1024).rearrange("p (h c) -> p h c", h=8, c=P)
    Qf = scr16(1, 1024, 1024).rearrange("p (h c) -> p h c", h=8, c=P)
    Qb = scr16(1, 1536, 1024).rearrange("p (h c) -> p h c", h=8, c=P)
    Qfs = [Qf, scr16(0, 0, 1024).rearrange("p (h c) -> p h c", h=8, c=P)]
    Qbs = [Qb, scr16(0, 512, 1024).rearrange("p (h c) -> p h c", h=8, c=P)]
    onorm = scr32(2, 0, 1024)
    retb = scr16(2, 1024, 1024)
    gtmp = [scr32(2, 1536, 256), scr32(2, 1792, 256)]
    SCR_BLOCK = {"expT0": 0, "expT1": 0, "PT": 0, "rden": 1, "PrT": 1, "Qf0": 1, "Qb0": 1, "Qf1": 0, "Qb1": 0, "onorm": 2, "retb": 2,
                 "gtmp0": 2, "gtmp1": 2}
    scr_seen = set()

    def scw(name):
        if name not in scr_seen:
            scr_seen.add(name)
            return [name, ("x1blk", SCR_BLOCK[name])]
        return [name]

    ps_rr = [0]

    def psalloc(n=1):
        r = []
        for _ in range(n):
            r.append(ps_rr[0] % 8)
            ps_rr[0] += 1
        return r

    def PSK(i):
        return ("ps", i)

    cp_rr = [0]

    def copy_any(out, in_, reads, writes, eng=None):
        if eng is None:
            eng = "dve" if cp_rr[0] % 2 == 0 else "act"
            cp_rr[0] += 1
        if eng == "dve":
            pg.add("dve", lambda e, o=out, i=in_: e.tensor_copy(o, i), reads, writes)
        elif eng == "act":
            pg.add("act", lambda e, o=out, i=in_: e.activation(out=o, in_=i, func=AF.Copy), reads, writes)
        else:
            pg.add("pool", lambda e, o=out, i=in_: e.tensor_copy(o, i), reads, writes)

    def mm(out, lhsT, rhs, start, stop, reads, writes):
        pg.add("pe", lambda e, o=out, l=lhsT, r=rhs, s=start, t=stop: e.matmul(o, l, r, start=s, stop=t), reads, writes)

    def tp(out, in_, reads, writes):
        pg.add("pe", lambda e, o=out, i=in_: e.transpose(o, i, ident[:]), reads + ["ident"], writes)

    def dma(q, out, in_, reads, writes, chan):
        pg.add(q, lambda e, o=out, i=in_: e.dma_start(out=o, in_=i), reads, writes, chan=chan)

    def act(out, in_, func, reads, writes, scale=None, bias=None):
        kw = {}
        if scale is not None:
            kw["scale"] = scale
        if bias is not None:
            kw["bias"] = bias
        pg.add("act", lambda e, o=out, i=in_, f=func, k=kw: e.activation(out=o, in_=i, func=f, **k), reads, writes)

    bar_t = sb("bar_t", [P, 8], F32)
    bar_n = [0]

    def barrier(keys):
        k = ("BAR", bar_n[0])
        bar_n[0] += 1
        pg.add("dve", lambda e: e.memset(bar_t[:], 0.0), [], list(keys) + [k])
        for eng in ["pe", "act", "pool", "sp"]:
            pg.add(eng, lambda e: e.nop(), [k], [])

    ring_n = [0]
    bg_hook = [None]

    def wload(g, ncols=4096):
        if bg_hook[0] is not None:
            bg_hook[0]()
        s = ring_n[0] % 3
        ring_n[0] += 1
        dma("sp", ring[s][:, 0:ncols], Ws[g, :, 0:ncols], [("Ws", g)], [("ring", s)], "ring%d" % s)
        return ring[s], ("ring", s)

    cstage = x1f[:, 0:C_END]
    dma("sp", cstage, cst_d[:, :], [], ["cstage"], "ld_c0")
    dma("sp", prm[:], prm_d[:, :], [], ["prm"], "ld_c1")
    dma("sp", gnb[:], gnb_d[:, :], [], ["gnb"], "ld_c2")
    dma("sp", lnb[:].rearrange("p a d -> p (a d)"), lnb_d[:, :], [], ["lnb"], "ld_c3")
    pg.add("dve", lambda e: e.memset(ones[:], 1.0), [], ["ones"])
    pg.add("dve", lambda e: e.memset(ones512[:], 1.0), [], ["ones"])
    pg.add("dve", lambda e: e.memset(S[:], 0.0), [], ["S"])
    pg.add("dve", lambda e: e.memset(Sfb[:], 0.0), [], ["Sfb"])
    pg.add("dve", lambda e: e.tensor_copy(ident[:], cstage[:, C_ID:C_ID + P]), ["cstage"], ["ident"])
    pg.add("dve", lambda e: e.tensor_copy(biasT[:].rearrange("p a b c -> p (a b c)"), cstage[:, C_BIAS:C_BIAS + 3072]),
           ["cstage"], ["biasT"])
    act(tab[:, 0:16], prm[:, 0:16], AF.Exp, ["prm"], ["tab"], scale=-1.0)
    act(tab[:, 0:16], tab[:, 0:16], AF.Ln, ["tab"], ["tab"], bias=1.0)
    pg.add("dve", lambda e: e.tensor_scalar(out=tab[:, 0:16], in0=tab[:, 0:16], scalar1=-1.0, scalar2=None, op0=ALU.mult),
           ["tab"], ["tab"])
    act(tab[:, TB_GCF:TB_GCF + 16], tab[:, 0:16], AF.Exp, ["tab"], ["tab"], scale=128.0)
    act(tab[:, TB_ES:TB_ES + 8], prm[:, 16:24], AF.Exp, ["prm", "tab"], ["tab"])
    pg.add("dve", lambda e: e.tensor_scalar(out=tab[:, TB_KDF:TB_KDF + 8], in0=tab[:, TB_LGF:TB_LGF + 8],
                                            scalar1=cstage[:, C_C127:C_C127 + 1], scalar2=LNS, op0=ALU.mult, op1=ALU.add),
           ["tab", "cstage"], ["tab"])
    pg.add("dve", lambda e: e.tensor_scalar(out=tab[:, TB_KDB:TB_KDB + 8], in0=tab[:, TB_LGB:TB_LGB + 8],
                                            scalar1=cstage[:, C_CJ:C_CJ + 1], scalar2=LNS, op0=ALU.mult, op1=ALU.add),
           ["tab", "cstage"], ["tab"])
    act(tab[:, TB_KDF:TB_KDF + 16], tab[:, TB_KDF:TB_KDF + 16], AF.Exp, ["tab"], ["tab"])
    pg.add("dve", lambda e: e.tensor_copy(tab[:, TB_LINK:TB_LINK + 1], prm[:, 24:25]), ["prm", "tab"], ["tab"])
    pg.add("dve", lambda e: e.tensor_scalar(out=tab[:, TB_LB:TB_LB + 1], in0=prm[:, 24:25], scalar1=-1.0, scalar2=-NEG,
                                            op0=ALU.add, op1=ALU.mult), ["prm", "tab"], ["tab"])
    tmpA = x1f[:, 4096:4096 + P]
    tmpB = x1f[:, 4096 + P:4096 + 2 * P]
    for h in range(8):
        pg.add("dve", lambda e, h=h: e.tensor_scalar(out=tmpA, in0=cstage[:, C_RP:C_RP + P], scalar1=tab[:, TB_LGF + h:TB_LGF + h + 1],
                                                     scalar2=None, op0=ALU.mult), ["cstage", "tab", "tmpB"], ["tmpA"])
        pg.add("dve", lambda e, h=h: e.scalar_tensor_tensor(out=tmpB, in0=cstage[:, C_RN:C_RN + P],
                                                            scalar=tab[:, TB_LGB + h:TB_LGB + h + 1], in1=tmpA,
                                                            op0=ALU.mult, op1=ALU.add), ["cstage", "tab", "tmpA"], ["tmpB"])
        act(Dcomb[:, h, :], tmpB, AF.Exp, ["tmpB"], ["Dcomb"], bias=None)
        act(AFt[:, h, :], cstage[:, C_IP1:C_IP1 + P], AF.Exp, ["cstage", "tab"], ["AFt"], scale=tab[:, TB_LGF + h:TB_LGF + h + 1])
        act(ABt[:, h, :], cstage[:, C_CMI:C_CMI + P], AF.Exp, ["cstage", "tab"], ["ABt"], scale=tab[:, TB_LGB + h:TB_LGB + h + 1])
        act(esr[:, h, :], ones[:], AF.Identity, ["ones", "tab"], ["esr"], scale=tab[:, TB_ES + h:TB_ES + h + 1])
    act(lbrow[:], ones[:], AF.Identity, ["ones", "tab"], ["lbrow"], scale=tab[:, TB_LB:TB_LB + 1])
    pg.add("dve", lambda e: e.tensor_scalar(out=Dcomb[:].rearrange("p a b -> p (a b)"), in0=Dcomb[:].rearrange("p a b -> p (a b)"),
                                            scalar1=SCALE, scalar2=None, op0=ALU.mult), ["Dcomb"], ["Dcomb"])
    barrier(["cstage", "tmpA", "tmpB", "stage0", "stage1", "tab", "Dcomb", "AFt", "ABt", "esr", "ident", "biasT", "ones"])

    w_in_v = w_in_d.rearrange("(kc p) c -> p kc c", p=P)
    w_out_v = w_out_d.rearrange("(kc p) c -> p kc c", p=P)
    w_fi_v = w_fi_d.rearrange("(kc p) c -> p kc c", p=P)
    w_fo_v = w_fo_d.rearrange("(kc p) c -> p kc c", p=P)
    FM_WIN = {0, 1, 2, 3, 4, 6, 7, 8, 9}

    def p0_load(g, seq, q="sp", chp="p0l"):
        s = seq % 2
        st3 = stage[s].rearrange("p (kc c) -> p kc c", kc=KC, c=256)
        key = "stage%d" % s
        ch = "%s%d" % (chp, s)
        if g < G_WOUT:
            dma(q, st3, w_in_v[:, :, g * 256:(g + 1) * 256], [], [key], ch)
        elif g < G_FIN:
            c0 = (g - G_WOUT) * 256
            dma(q, st3, w_out_v[:, :, c0:c0 + 256], [], [key], ch)
        elif g < G_FOUT:
            c = g - G_FIN
            dma(q, st3[:, :, 0:128], w_fi_v[:, :, c * 128:(c + 1) * 128], [], [key], ch)
            dma(q, st3[:, :, 128:256], w_fi_v[:, :, FH + c * 128:FH + (c + 1) * 128], [], [key], ch)
        else:
            cg, piece = divmod(g - G_FOUT, 4)
            st11 = stage[s][:, 0:2816].rearrange("p (kc c) -> p kc c", kc=11, c=256)
            dma(q, st11, w_fo_v[:, piece * 11:(piece + 1) * 11, cg * 256:(cg + 1) * 256], [], [key], ch)

    def p0_cast_store(g, seq, background=False):
        s = seq % 2
        r = seq % 3
        key = "stage%d" % s
        fm = (g in FM_WIN) or (G_FIN <= g < G_FOUT)
        if background:
            eng, dst, dkey, ch = "pool", cvb, "cvb", "p0sb"
        else:
            eng, dst, dkey, ch = ("dve" if seq % 2 == 0 else "act"), ring[r][:], ("ring", r), "p0s%d" % r
        if g >= G_FOUT:
            o = dst[:, 0:2816]
            i = stage[s][:, 0:2816]
            ncols = 2816
        elif fm:
            o = dst.rearrange("p (cc kc j) -> p kc cc j", cc=2, kc=KC, j=P)
            i = stage[s].rearrange("p (kc cc j) -> p kc cc j", kc=KC, cc=2, j=P)
            ncols = 4096
        else:
            o = dst
            i = stage[s]
            ncols = 4096
        copy_any(o, i, [key], [dkey], eng=eng)
        dma("pool", Ws[g, :, 0:ncols], dst[:, 0:ncols], [dkey], [("Ws", g)], ch)

    P0_FIRST = [4, 5] + list(range(10, 18))
    p0_rest = [g for g in range(G_TOT) if g not in P0_FIRST]
    if STAGE >= 2:
        for i in range(len(P0_FIRST) + 1):
            if i < len(P0_FIRST):
                p0_load(P0_FIRST[i], i)
            if i >= 1:
                p0_cast_store(P0_FIRST[i - 1], i - 1)

    def p0_gen(gs, base):
        for i in range(len(gs) + 1):
            if i < len(gs):
                p0_load(gs[i], base + i, q="pool", chp="p0lb")
            if i >= 1:
                p0_cast_store(gs[i - 1], base + i - 1, background=True)
            yield

    bg = {"it": None, "per": 0.0, "acc": 0.0}

    def bg_tick():
        if bg["it"] is None:
            return
        bg["acc"] += bg["per"]
        while bg["acc"] >= 1.0:
            bg["acc"] -= 1.0
            try:
                next(bg["it"])
            except StopIteration:
                bg["it"] = None
                return

    barrier(["stage0", "stage1"] + [("ring", r) for r in range(3)] + [("Ws", g) for g in P0_FIRST])

    def load_xT(n, tb, slot):
        dma("pool", xbf[slot][:], x_d[n * P:(n + 1) * P, :], [], [("xbf", slot)], "xbf%d" % slot)
        for half in range(2):
            b = psalloc()[0]
            pv = psb[b][:].bitcast(BF16).rearrange("p (k c) -> p k c", k=8, c=P)
            for k in range(8):
                kc = half * 8 + k
                tp(pv[:, k, :], xbf[slot][:, kc * P:(kc + 1) * P], [("xbf", slot)], [PSK(b)])
            copy_any(xT[:, half * 8:(half + 1) * 8, tb * P:(tb + 1) * P], pv, [PSK(b)], [("xT", tb)])

    XT_ALL = [("xT", tb) for tb in range(NB)]

    def seg_last(n):
        return (n % SEGB == SEGB - 1) and n != NBLK - 1

    def seg_first(n):
        return (n % SEGB == 0) and n != 0

    xslot = [0]
    def p1_A(ti):
        n0 = ti * NB
        for tb in range(NB):
            load_xT(n0 + tb, tb, xslot[0] % 2)
            xslot[0] += 1
        wt, wk = wload(4)
        wv = wt[:].rearrange("p (cc kc j) -> p cc kc j", cc=2, kc=KC, j=P)
        for cc in range(2):
            b = psalloc()[0]
            for kc in range(KC):
                mm(psb[b][:], wv[:, cc, kc, :], xT[:, kc, :], kc == 0, kc == KC - 1, [wk] + XT_ALL, [PSK(b)])
            act(attrec[:, :, cc * P:(cc + 1) * P], psb[b][:].rearrange("p (b c) -> p b c", b=NB, c=P), AF.Copy, [PSK(b)], ["attasm"],
                scale=SCALE)
        wt, wk = wload(5)
        wv = wt[:].rearrange("p (kc c) -> p kc c", kc=KC, c=256)
        bs = psalloc(4)
        for tb in range(NB):
            o = psb[bs[tb]][:, 0:256]
            for kc in range(KC):
                mm(o, xT[:, kc, tb * P:(tb + 1) * P], wv[:, kc, :], kc == 0, kc == KC - 1, [wk, ("xT", tb)], [PSK(bs[tb])])
        for tb in range(NB):
            o = psb[bs[tb]][:, 0:256]
            copy_any(attrec[:, tb, 256:512], o, [PSK(bs[tb])], ["attasm"])
        dma("pool", AttRec[n0:n0 + NB].rearrange("n p c -> p n c"), attrec[:], ["attasm"], [("AttRec", ti)], "arst")

    def p1_B(ti):
        n0 = ti * NB
        for part in range(2):
            for cgi in range(4):
                wt, wk = wload(10 + part * 4 + cgi)
                wv = wt[:].rearrange("p (kc c) -> p kc c", kc=KC, c=256)
                bs = psalloc(4)
                for kc in range(KC):
                    for tb in range(NB):
                        o = psb[bs[tb]][:, 0:256]
                        mm(o, xT[:, kc, tb * P:(tb + 1) * P], wv[:, kc, :], kc == 0, kc == KC - 1, [wk, ("xT", tb)], [PSK(bs[tb])])
                for tb in range(NB):
                    bk = PSK(bs[tb])
                    o = psb[bs[tb]][:, 0:256]
                    cs = slice(cgi * 256, (cgi + 1) * 256)
                    if part == 1:
                        copy_any(rec1[:, tb, 2, cs], o, [bk], [("rec1v", tb, cgi)])
                    else:
                        copy_any(krb[:, tb, cs], o, [bk], [("krb", tb, cgi)])
                        for hh in range(2):
                            h = cgi * 2 + hh
                            hs = slice(h * P, (h + 1) * P)
                            oh = o[:, hh * P:(hh + 1) * P]
                            pg.add("dve", lambda e, tb=tb, hs=hs, oh=oh, h=h: e.tensor_scalar(
                                out=rec1[:, tb, 1, hs], in0=oh, scalar1=tab[:, TB_KDF + h:TB_KDF + h + 1], scalar2=None, op0=ALU.mult),
                                [bk, "tab"], [("rec1k", tb, cgi)])
                            pg.add("dve", lambda e, tb=tb, hs=hs, oh=oh, h=h: e.tensor_scalar(
                                out=kdb[:, tb, hs], in0=oh, scalar1=tab[:, TB_KDB + h:TB_KDB + h + 1], scalar2=None, op0=ALU.mult),
                                [bk, "tab"], [("kdb", tb, cgi)])
        for tb in range(NB):
            b = psalloc()[0]
            pv = psb[b][:].bitcast(BF16).rearrange("p (k c) -> p k c", k=8, c=P)
            for h in range(8):
                tp(pv[:, h, :], krb[:, tb, h * P:(h + 1) * P], [("krb", tb, h // 2)], [PSK(b)])
            copy_any(rec1[:, tb, 0, :].rearrange("p (k c) -> p k c", k=8, c=P), pv, [PSK(b)], [("rec1t", tb)])

    def p1_C(ti):
        n0 = ti * NB
        for tb in range(NB - 1, -1, -1):
            n = n0 + tb
            if seg_last(n):
                pg.add("dve", lambda e: e.tensor_scalar(out=S[:], in0=S[:], scalar1=tab[:, TB_LINK:TB_LINK + 1], scalar2=None,
                                                        op0=ALU.mult), ["S", "tab"], ["S"])
            copy_any(rec1[:, tb, 3, :], S[:], ["S"], [("rec1s", tb)], eng="act")
            bs = psalloc(2)
            for h in range(8):
                o = psb[bs[h // 4]][:, (h % 4) * P:(h % 4 + 1) * P]
                mm(o, kdb[:, tb, h * P:(h + 1) * P], rec1[:, tb, 2, h * P:(h + 1) * P], True, True,
                   [("kdb", tb, h // 2), ("rec1v", tb, h // 2)], [PSK(bs[h // 4])])
            for h in range(8):
                o = psb[bs[h // 4]][:, (h % 4) * P:(h % 4 + 1) * P]
                pg.add("dve", lambda e, h=h, o=o: e.scalar_tensor_tensor(
                    out=S[:, h * P:(h + 1) * P], in0=S[:, h * P:(h + 1) * P], scalar=tab[:, TB_GCB + h:TB_GCB + h + 1], in1=o,
                    op0=ALU.mult, op1=ALU.add), ["S", "tab", PSK(bs[h // 4])], ["S"])
            rk = [("rec1t", tb), ("rec1s", tb)] + [("rec1k", tb, c) for c in range(4)] + [("rec1v", tb, c) for c in range(4)]
            dma("pool", RetRec[n], rec1[:, tb].rearrange("p q c -> p (q c)"), rk, [("RetRec", n)], "rrst%d" % tb)

    if STAGE >= 3:
        bg["it"] = p0_gen(p0_rest, len(P0_FIRST))
        bg["per"] = (len(p0_rest) + 1) / float(NT * 10 - 4)
        bg_hook[0] = bg_tick
        p1_A(NT - 1)
        p1_B(NT - 1)
        for ti in range(NT - 2, -1, -1):
            p1_A(ti)
            p1_C(ti + 1)
            p1_B(ti)
        p1_C(0)
    bg_hook[0] = None
    if bg["it"] is not None:
        for _ in bg["it"]:
            pass
        bg["it"] = None
    bk = ["S", "attasm"] + [("RetRec", n) for n in range(NBLK)] + [("AttRec", t) for t in range(NT)]
    for tb in range(NB):
        bk += [("rec1t", tb), ("rec1s", tb)] + [("rec1k", tb, c) for c in range(4)] + [("rec1v", tb, c) for c in range(4)]
        bk += [("krb", tb, c) for c in range(4)] + [("kdb", tb, c) for c in range(4)] + [("xT", tb)]
    barrier(bk + [("ring", r) for r in range(3)] + ["stage0", "stage1", "cvb"] + [("Ws", g) for g in range(G_TOT)])
    pg.add("dve", lambda e: e.memset(S[:], 0.0), ["S"], ["S"])

    lst6 = sb("lst6", [P, NB, 8, 6], F32)
    lmv = sb("lmv", [P, NB, 2], F32)
    lsd = sb("lsd", [P, NB], F32)
    lrs = sb("lrs", [P, NB], F32)
    lnbb = sb("lnbb", [P, NB], F32)
    PTs = [PT, scr16(3, 0, 1536).rearrange("p (k c) -> p k c", k=3, c=512)]
    rdens = [rden, scr32(3, 768, 512)]
    SCR_BLOCK.update({"PT0": 0, "PT1": 3, "rden0": 1, "rden1": 3})

    def evac_resid(bs, cgi):
        for tb in range(NB):
            o = psb[bs[tb]][:, 0:256]
            xs = x1buf[:, tb, cgi * 256:(cgi + 1) * 256]
            pg.add("dve", lambda e, o=o, xs=xs: e.scalar_tensor_tensor(out=xs, in0=xs, scalar=ALPHA, in1=o, op0=ALU.mult, op1=ALU.add),
                   [PSK(bs[tb]), ("x1", tb, cgi)], [("x1", tb, cgi)])
            pg.add("dve", lambda e, tb=tb, cgi=cgi, xs=xs: e.bn_stats(out=lst6[:, tb, cgi, :], in_=xs), [("x1", tb, cgi)], [("lst6", tb, cgi)])

    def ln_steps(which, after):
        steps = []

        def s0():
            for tb in range(NB):
                pg.add("dve", lambda e, tb=tb: e.bn_aggr(out=lmv[:, tb, :], in_=lst6[:, tb].rearrange("p a b -> p (a b)")),
                       [("lst6", tb, c) for c in range(8)], [("lmv", tb)])
            lk = [("lmv", tb) for tb in range(NB)]
            act(lsd[:], lmv[:, :, 1], AF.Ln, lk, ["lsd"], bias=EPS)
            act(lrs[:], lsd[:], AF.Exp, ["lsd"], ["lrs"], scale=-0.5)
        steps.append(s0)
        for tb in range(NB):
            def st(tb=tb):
                xkeys = [("x1", tb, c) for c in range(8)]
                xv = x1buf[:, tb, :]
                pg.add("dve", lambda e, xv=xv, tb=tb: e.scalar_tensor_tensor(
                    out=xv, in0=xv, scalar=lmv[:, tb, 0:1], in1=lnb[:, 2 * which, :], op0=ALU.subtract, op1=ALU.mult),
                    xkeys + ["lnb", ("lmv", tb)], xkeys)
                pg.add("dve", lambda e, xv=xv, tb=tb: e.scalar_tensor_tensor(
                    out=xv, in0=xv, scalar=lrs[:, tb:tb + 1], in1=lnb[:, 2 * which + 1, :], op0=ALU.mult, op1=ALU.add),
                    xkeys + ["lnb", "lrs"], xkeys)
                after(tb, xkeys)
            steps.append(st)
        return steps

    def ln_finish(which, after):
        for s in ln_steps(which, after):
            s()

    def front1(ti):
        n0 = ti * NB
        for tb in range(NB):
            load_xT(n0 + tb, tb, xslot[0] % 2)
            xslot[0] += 1
        proj_q(0, QaT, "QaT")

    def proj_q(g0, dst, nm, steps=None):
        for gi in range(4):
            wt, wk = wload(g0 + gi)
            wv = wt[:].rearrange("p (cc kc j) -> p cc kc j", cc=2, kc=KC, j=P)
            for cc in range(2):
                c = gi * 2 + cc
                b = psalloc()[0]
                for kc in range(KC):
                    mm(psb[b][:], wv[:, cc, kc, :], xT[:, kc, :], kc == 0, kc == KC - 1, [wk] + XT_ALL, [PSK(b)])
                copy_any(dst[:, c, :], psb[b][:], [PSK(b)], [(nm, c)])
                if steps:
                    steps.pop(0)()

    def front2(ti, steps=None):
        scr_seen.clear()
        proj_q(6, QrT, "QrT", steps)
        while steps:
            steps.pop(0)()
        for cgi in range(4):
            wt, wk = wload(18 + cgi)
            wv = wt[:].rearrange("p (kc c) -> p kc c", kc=KC, c=256)
            bs = psalloc(4)
            for kc in range(KC):
                for tb in range(NB):
                    mm(psb[bs[tb]][:, 0:256], xT[:, kc, tb * P:(tb + 1) * P], wv[:, kc, :], kc == 0, kc == KC - 1,
                       [wk, ("xT", tb)], [PSK(bs[tb])])
            for tb in range(NB):
                gi_ = tb % 2
                gname = "gtmp%d" % gi_
                act(gtmp[gi_], psb[bs[tb]][:, 0:256], AF.Silu, [PSK(bs[tb])], scw(gname))
                pg.add("pool", lambda e, tb=tb, cgi=cgi, gi_=gi_: e.tensor_tensor(
                    out=gsg[:, tb, cgi * 256:(cgi + 1) * 256], in0=gtmp[gi_], in1=gnb[:, cgi * 256:(cgi + 1) * 256], op=ALU.mult),
                    [gname, "gnb"], [("gsg", tb)])

    cyc_n = [0]

    def cyc():
        cyc_n[0] += 1
        return cyc_n[0] % 2

    def blk_ctx(ti, tb):
        n = ti * NB + tb
        return {"n": n, "tb": tb, "ts": slice(tb * P, (tb + 1) * P), "rs": n % 2,
                "kbs": [kb for kb in range(3) if 0 <= n + kb - 1 < NBLK]}

    def blk_E1(c):
        n, tb, ts_, rs, kbs = c["n"], c["tb"], c["ts"], c["rs"], c["kbs"]
        if n == 0:
            dma("pool", attrec[:, 0, :], AttRec[0], [("AttRec", 0)], [("attrec", 0)], "arld0")
        if n + 1 < NBLK:
            a_ = (n + 1) % 4
            dma("pool", attrec[:, a_, :], AttRec[n + 1], [("AttRec", (n + 1) // NB)], [("attrec", a_)], "arld%d" % a_)
        rk = ("retrec", rs)
        rec = retrec[rs]
        qrk = [("QrT", q) for q in range(8)]
        qi = n % 2
        pg.add("pool", lambda e, ts_=ts_, qi=qi: e.tensor_tensor(out=Qfs[qi], in0=QrT[:, :, ts_], in1=AFt[:], op=ALU.mult),
               qrk + ["AFt"], scw("Qf%d" % qi))
        pg.add("pool", lambda e, ts_=ts_, qi=qi: e.tensor_tensor(out=Qbs[qi], in0=QrT[:, :, ts_], in1=ABt[:], op=ALU.mult),
               qrk + ["ABt"], scw("Qb%d" % qi))
        for q in range(2):
            b = cyc()
            for hh in range(4):
                h = q * 4 + hh
                mm(psb[b][:, hh * P:(hh + 1) * P], rec[:, 0, h * P:(h + 1) * P], QrT[:, h, ts_], True, True, [rk, ("QrT", h)], [PSK(b)])
            pg.add("dve", lambda e, q=q, b=b: e.tensor_tensor(
                out=PrT[:, q * 4:(q + 1) * 4, :], in0=psb[b][:].rearrange("p (g c) -> p g c", g=4, c=P),
                in1=Dcomb[:, q * 4:(q + 1) * 4, :], op=ALU.mult), [PSK(b), "Dcomb"], scw("PrT"))
        for kv in range(2):
            hs4 = slice(kv * 4, (kv + 1) * 4)
            qk = [("QaT", q) for q in range(kv * 4, kv * 4 + 4)]
            ptn = "PT%d" % kv
            for ki, kb in enumerate(kbs):
                a_ = (n + kb - 1) % 4
                b = cyc()
                cross = (kb == 0 and seg_first(n)) or (kb == 2 and seg_last(n))
                mm(psb[b][:].rearrange("p (g c) -> p g c", g=4, c=P), attrec[:, a_, kv * P:(kv + 1) * P], QaT[:, hs4, ts_],
                   True, False, [("attrec", a_)] + qk, [PSK(b)])
                mm(psb[b][:], ident[:], biasT[:, kb, hs4, :].rearrange("p a b -> p (a b)"), False, not cross, ["ident", "biasT"], [PSK(b)])
                if cross:
                    mm(psb[b][:], lbrow[0:1, :], ones512[0:1, :], False, True,
                       ["lbrow", "ones"], [PSK(b)])
                act(PTs[kv][:, ki, :], psb[b][:], AF.Exp, [PSK(b)], scw(ptn))

    def blk_E2(c):
        n, tb, ts_, rs, kbs = c["n"], c["tb"], c["ts"], c["rs"], c["kbs"]
        rk = ("retrec", rs)
        rec = retrec[rs]
        for kv in range(2):
            hs4 = slice(kv * 4, (kv + 1) * 4)
            ptn = "PT%d" % kv
            rn = "rden%d" % kv
            bo = 2 + kv
            bd = cyc()
            for ki, kb in enumerate(kbs):
                a_ = (n + kb - 1) % 4
                mm(psb[bo][:], attrec[:, a_, 256 + kv * P:256 + (kv + 1) * P], PTs[kv][:, ki, :], ki == 0, ki == len(kbs) - 1,
                   [("attrec", a_), ptn], [PSK(bo)])
            for ki, kb in enumerate(kbs):
                mm(psb[bd][:], ones[:], PTs[kv][:, ki, :], ki == 0, False, ["ones", ptn], [PSK(bd)])
            mm(psb[bd][:], ones[0:1, :], esr[0:1, hs4, :].rearrange("p a b -> p (a b)"), False, True, ["ones", "esr"], [PSK(bd)])
            act(rdens[kv], psb[bd][:], AF.Ln, [PSK(bd)], scw(rn))
            act(rdens[kv], rdens[kv], AF.Exp, [rn], [rn], scale=-1.0)
        rb = 4 + 2 * (n % 2)
        c["rb"] = rb
        for h in range(8):
            bo = rb + h // 4
            o = psb[bo][:, (h % 4) * P:(h % 4 + 1) * P]
            hs = slice(h * P, (h + 1) * P)
            mm(o, PrT[:, h, :], rec[:, 2, hs], True, False, ["PrT", rk], [PSK(bo)])
            mm(o, Qfs[n % 2][:, h, :], Sfb[:, hs], False, False, ["Qf%d" % (n % 2), "Sfb"], [PSK(bo)])
            mm(o, Qbs[n % 2][:, h, :], rec[:, 3, hs], False, True, ["Qb%d" % (n % 2), rk], [PSK(bo)])
        sbk = [cyc(), cyc()]
        for h in range(8):
            bo = sbk[h // 4]
            o = psb[bo][:, (h % 4) * P:(h % 4 + 1) * P]
            hs = slice(h * P, (h + 1) * P)
            mm(o, rec[:, 1, hs], rec[:, 2, hs], True, True, [rk], [PSK(bo)])
        for h in range(8):
            o = psb[sbk[h // 4]][:, (h % 4) * P:(h % 4 + 1) * P]
            pg.add("dve", lambda e, h=h, o=o: e.scalar_tensor_tensor(
                out=S[:, h * P:(h + 1) * P], in0=S[:, h * P:(h + 1) * P], scalar=tab[:, TB_GCF + h:TB_GCF + h + 1], in1=o,
                op0=ALU.mult, op1=ALU.add), ["S", "tab", PSK(sbk[h // 4])], ["S"])
        if seg_last(n):
            pg.add("dve", lambda e: e.tensor_scalar(out=S[:], in0=S[:], scalar1=tab[:, TB_LINK:TB_LINK + 1], scalar2=None,
                                                    op0=ALU.mult), ["S", "tab"], ["S"])
        copy_any(Sfb[:], S[:], ["S"], ["Sfb"], eng="act")

    def blk_late(c):
        n, tb, ts_ = c["n"], c["tb"], c["ts"]
        for kv in range(2):
            hs4 = slice(kv * 4, (kv + 1) * 4)
            bo = 2 + kv
            pg.add("dve", lambda e, bo=bo, hs4=hs4, kv=kv: e.tensor_tensor(
                out=mixT[:, hs4, ts_], in0=psb[bo][:].rearrange("p (g c) -> p g c", g=4, c=P),
                in1=rdens[kv].rearrange("p (g c) -> p g c", g=4, c=P), op=ALU.mult), [PSK(bo), "rden%d" % kv], [("mixT", kv, tb)])
        rb = c["rb"]
        for h in range(8):
            o = psb[rb + h // 4][:, (h % 4) * P:(h % 4 + 1) * P]
            pg.add("dve", lambda e, h=h, o=o: e.bn_stats(out=st6[:, h, :], in_=o), [PSK(rb + h // 4)], ["st6"])
        for h in range(8):
            pg.add("dve", lambda e, h=h: e.bn_aggr(out=mv[:, h, :], in_=st6[:, h, :]), ["st6"], ["mv"])
        act(sd[:], mv[:, :, 1], AF.Ln, ["mv"], ["sd"], bias=EPS)
        act(rstd[:], sd[:], AF.Exp, ["sd"], ["rstd"], scale=-0.5)
        for h in range(8):
            o = psb[rb + h // 4][:, (h % 4) * P:(h % 4 + 1) * P]
            pg.add("dve", lambda e, h=h, o=o: e.tensor_scalar(
                out=onorm[:, h * P:(h + 1) * P], in0=o, scalar1=mv[:, h, 0:1], scalar2=rstd[:, h:h + 1],
                op0=ALU.subtract, op1=ALU.mult), [PSK(rb + h // 4), "rstd", "mv"], scw("onorm"))
        pg.add("pool", lambda e: e.tensor_tensor(out=retb, in0=onorm, in1=gsg[:, tb, :], op=ALU.mult),
               ["onorm", ("gsg", tb)], scw("retb"))

    def blk_lateB(c):
        n, tb, ts_ = c["n"], c["tb"], c["ts"]
        b = cyc()
        pv = psb[b][:].bitcast(BF16).rearrange("p (k c) -> p k c", k=8, c=P)
        for h in range(8):
            tp(pv[:, h, :], retb[:, h * P:(h + 1) * P], ["retb"], [PSK(b)])
        copy_any(mixT[:, 8:16, ts_], pv, [PSK(b)], [("mixT", 2, tb)], eng="act")

    def rec_load(c):
        n, rs = c["n"], c["rs"]
        dma("pool", retrec[rs][:].rearrange("p q c -> p (q c)"), RetRec[n], [("RetRec", n)], [("retrec", rs)], "rrld%d" % rs)

    def attn_ret_tile(ti):
        cs = [blk_ctx(ti, tb) for tb in range(NB)]
        rec_load(cs[0])
        rec_load(cs[1])
        blk_E1(cs[0])
        blk_E2(cs[0])
        rec_load(cs[2])
        for tb in range(1, NB):
            blk_E1(cs[tb])
            blk_late(cs[tb - 1])
            blk_E2(cs[tb])
            if tb + 2 < NB:
                rec_load(cs[tb + 2])
            blk_lateB(cs[tb - 1])
        blk_late(cs[NB - 1])
        blk_lateB(cs[NB - 1])

    def mid(ti):
        n0 = ti * NB
        attn_ret_tile(ti)
        allscr = list(SCR_BLOCK.keys())
        for tb in range(NB):
            wk_ = [("x1", tb, c) for c in range(8)] + [("x1blk", tb)] + [k for k in allscr if SCR_BLOCK[k] == tb]
            dma("pool", x1buf[:, tb, :], x_d[(n0 + tb) * P:(n0 + tb + 1) * P, :], [], wk_, "x1ld%d" % tb)
        for cgi in range(8):
            wt, wk = wload(G_WOUT + cgi)
            wv = wt[:].rearrange("p (kc c) -> p kc c", kc=KC, c=256)
            bs = psalloc(4)
            for kc in range(KC):
                grp = 0 if kc < 4 else (1 if kc < 8 else 2)
                for tb in range(NB):
                    mm(psb[bs[tb]][:, 0:256], mixT[:, kc, tb * P:(tb + 1) * P], wv[:, kc, :], kc == 0, kc == KC - 1,
                       [wk, ("mixT", grp, tb)], [PSK(bs[tb])])
            evac_resid(bs, cgi)

        def after1(tb, xkeys):
            copy_any(xb1[:], x1buf[:, tb, :], xkeys, [("xbf", 0)], eng="act")
            for half in range(2):
                b = psalloc()[0]
                pv = psb[b][:].bitcast(BF16).rearrange("p (k c) -> p k c", k=8, c=P)
                for k in range(8):
                    kc = half * 8 + k
                    tp(pv[:, k, :], xb1[:, kc * P:(kc + 1) * P], [("xbf", 0)], [PSK(b)])
                copy_any(xT[:, half * 8:(half + 1) * 8, tb * P:(tb + 1) * P], pv, [PSK(b)], [("xT", tb)])
        ln_finish(0, after1)
        for c in range(FCH):
            wt, wk = wload(G_FIN + c)
            wv = wt[:].rearrange("p (cc kc j) -> p cc kc j", cc=2, kc=KC, j=P)
            bg, bu = psalloc(2)
            for kc in range(KC):
                mm(psb[bg][:], wv[:, 0, kc, :], xT[:, kc, :], kc == 0, kc == KC - 1, [wk] + XT_ALL, [PSK(bg)])
            for kc in range(KC):
                mm(psb[bu][:], wv[:, 1, kc, :], xT[:, kc, :], kc == 0, kc == KC - 1, [wk] + XT_ALL, [PSK(bu)])
            fi = c % 2
            act(ftmp[fi][:], psb[bg][:], AF.Silu, [PSK(bg)], [("ftmp", fi)])
            pg.add("dve", lambda e, c=c, fi=fi, bu=bu: e.tensor_tensor(out=hT[:, c, :], in0=ftmp[fi][:], in1=psb[bu][:], op=ALU.mult),
                   [("ftmp", fi), PSK(bu)], [("hT", c)])
        for cgi in range(8):
            bs = psalloc(4)
            for piece in range(4):
                wt, wk = wload(G_FOUT + cgi * 4 + piece, 2816)
                wv = wt[:, 0:2816].rearrange("p (kc c) -> p kc c", kc=11, c=256)
                for kk in range(11):
                    c = piece * 11 + kk
                    for tb in range(NB):
                        mm(psb[bs[tb]][:, 0:256], hT[:, c, tb * P:(tb + 1) * P], wv[:, kk, :], c == 0, c == FCH - 1,
                           [wk, ("hT", c)], [PSK(bs[tb])])
            evac_resid(bs, cgi)

    def ln2_store(ti):
        n0 = ti * NB

        def after2(tb, xkeys):
            dma("pool", y_d[(n0 + tb) * P:(n0 + tb + 1) * P, :], x1buf[:, tb, :], xkeys + [("x1blk", tb)], [("y", n0 + tb)], "yst%d" % tb)
        return ln_steps(1, after2)

    if STAGE >= 4:
        front1(0)
        pend = None
        for ti in range(NT):
            front2(ti, pend)
            mid(ti)
            if ti + 1 < NT:
                front1(ti + 1)
                pend = ln2_store(ti)
            else:
                for s in ln2_store(ti):
                    s()


    pg.emit(nc, es, ["yst%d" % tb for tb in range(NB)])
    es.close()
    return nc


def make_consts():
    c = np.zeros((P, C_END), np.float32)
    j = np.arange(P)[:, None].astype(np.float32)
    i = np.arange(P)[None, :].astype(np.float32)
    c[:, C_RP:C_RP + P] = np.maximum(i - j, 0)
    c[:, C_RN:C_RN + P] = np.maximum(j - i, 0)
    c[:, C_IP1:C_IP1 + P] = i + 1
    c[:, C_CMI:C_CMI + P] = 128 - i
    c[:, C_C127] = 127 - np.arange(P)
    c[:, C_CJ] = np.arange(P)
    bias = np.zeros((P, 3, 8, P), np.float32)
    for kb in range(3):
        kpos = (kb - 1) * P + j
        dist = np.abs(i - kpos)
        for h in range(8):
            slope = 2.0 ** (-(h + 1))
            bias[:, kb, h, :] = np.where(dist <= P, -slope * dist, NEG)
    c[:, C_BIAS:C_BIAS + 3072] = bias.reshape(P, 3072)
    c[:, C_ID:C_ID + P] = np.eye(P, dtype=np.float32)
    return c


def _bc(v, n):
    return np.ascontiguousarray(np.broadcast_to(np.asarray(v, np.float32).reshape(1, -1), (P, n)))


_NC_CACHE = {}


def kernel(x_prompt, x_sample, w_in, attn_sink, ret_decay_fwd, ret_decay_bwd, ret_gn_gain, w_out,
           ln1_gain, ln1_bias, w_ffn_in, w_ffn_out, ln2_gain, ln2_bias):
    x_prompt = np.asarray(x_prompt, np.float32)
    x_sample = np.asarray(x_sample, np.float32)
    if "nc" not in _NC_CACHE:
        _NC_CACHE["nc"] = build(16, 16)
    nc = _NC_CACHE["nc"]
    cst = make_consts()
    gnb = _bc(ret_gn_gain, 1024)
    lnb = np.concatenate([_bc(ln1_gain, D), _bc(ln1_bias, D), _bc(ln2_gain, D), _bc(ln2_bias, D)], axis=1)
    shared = {
        "w_in": np.ascontiguousarray(np.asarray(w_in, np.float32)[0]),
        "w_out": np.ascontiguousarray(np.asarray(w_out, np.float32)[0]),
        "w_ffn_in": np.ascontiguousarray(np.asarray(w_ffn_in, np.float32)[0]),
        "w_ffn_out": np.ascontiguousarray(np.asarray(w_ffn_out, np.float32)[0]),
        "cst": cst, "gnb": gnb, "lnb": np.ascontiguousarray(lnb),
    }
    in_maps = []
    for core in range(8):
        prm = np.zeros((P, 32), np.float32)
        prm[:, 0:8] = _bc(ret_decay_fwd, 8)
        prm[:, 8:16] = _bc(ret_decay_bwd, 8)
        prm[:, 16:24] = _bc(attn_sink, 8)
        if core < 4:
            xc = x_prompt[core]
            prm[:, 24] = 1.0
        else:
            xc = x_sample[(core - 4) * 4:(core - 3) * 4].reshape(8192, D)
            prm[:, 24] = 0.0
        m = dict(shared)
        m["x"] = np.ascontiguousarray(xc)
        m["prm"] = prm
        in_maps.append(m)
    res = run_bass_kernel_spmd(nc, in_maps, core_ids=list(range(8)))
    ys = [np.asarray(r["y"], np.float32) for r in res.results]
    y_prompt = np.stack(ys[0:4], axis=0)
    y_sample = np.concatenate([y.reshape(4, 2048, D) for y in ys[4:8]], axis=0)
    return (y_prompt, y_sample)
```

```python
import contextlib
import os
import numpy as np
import concourse.bass as bass
import concourse.mybir as mybir
from concourse.bass_utils import run_bass_kernel_spmd

F32 = mybir.dt.float32
BF16 = mybir.dt.bfloat16
AF = mybir.ActivationFunctionType
ALU = mybir.AluOpType

P = 128
D = 2048
KC = 16
T = 512
NB = 4
IN_W = 5632
FH = 5632
FCH = 44
ALPHA = 2.0 ** 0.25
EPS = 1e-5
SCALE = 128.0 ** -0.5
LNS = float(np.log(SCALE))
NEG = -30000.0

C_RP, C_RN, C_IP1, C_CMI, C_C127, C_CJ, C_BIAS, C_ID, C_END = 0, 128, 256, 384, 512, 513, 514, 514 + 3072, 514 + 3072 + 128
TB_LGF, TB_LGB, TB_GCF, TB_GCB, TB_ES, TB_KDF, TB_KDB, TB_LINK, TB_LB = 0, 8, 16, 24, 32, 40, 48, 56, 57
G_WOUT, G_FIN, G_FOUT, G_TOT = 22, 30, 74, 106


class Op:
    __slots__ = ("eng", "fn", "deps", "chan", "sig", "val")

    def __init__(self, eng, fn, deps, chan):
        self.eng = eng
        self.fn = fn
        self.deps = deps
        self.chan = chan
        self.sig = chan is not None
        self.val = 0


class Prog:
    def __init__(self):
        self.ops = []
        self.lw = {}
        self.rd = {}

    def add(self, eng, fn, reads=(), writes=(), chan=None):
        idx = len(self.ops)
        ops = self.ops
        deps = set()
        lw = self.lw
        rd = self.rd
        for k in reads:
            w = lw.get(k)
            if w is not None:
                deps.add(w)
        for k in writes:
            w = lw.get(k)
            if w is not None:
                deps.add(w)
            r = rd.get(k)
            if r:
                deps.update(r.values())
        rkey = chan if chan is not None else eng
        for k in reads:
            r = rd.get(k)
            if r is None:
                rd[k] = {rkey: idx}
            else:
                r[rkey] = idx
        for k in writes:
            lw[k] = idx
            rd[k] = None
        if eng == "pe":
            deps = [d for d in deps if not (ops[d].eng == "pe" and ops[d].chan is None)]
        else:
            deps = list(deps)
        for d in deps:
            ops[d].sig = True
        ops.append(Op(eng, fn, deps, chan))
        return idx

    def emit(self, nc, es, final_chans):
        engs = ["pe", "dve", "act", "pool", "sp"]
        sems = {e: es.enter_context(nc.semaphore("s_" + e)) for e in engs}
        chans = {}
        cnt = {}
        for op in self.ops:
            if op.chan is not None:
                if op.chan not in chans:
                    chans[op.chan] = es.enter_context(nc.semaphore("c_" + str(op.chan)))
                    cnt[op.chan] = 0
                cnt[op.chan] += 16
                op.val = cnt[op.chan]
            elif op.sig:
                cnt[op.eng] = cnt.get(op.eng, 0) + 1
                op.val = cnt[op.eng]
        per = {e: [] for e in engs}
        for op in self.ops:
            per[op.eng].append(op)
        ops = self.ops

        def run(e, eobj):
            known = {}
            for op in per[e]:
                waits = {}
                for d in op.deps:
                    p = ops[d]
                    s = chans[p.chan] if p.chan is not None else sems[p.eng]
                    key = id(s)
                    if key not in waits or waits[key][1] < p.val:
                        waits[key] = (s, p.val)
                for key, (s, v) in waits.items():
                    if known.get(key, 0) >= v:
                        continue
                    eobj.wait_ge(s, v)
                    known[key] = v
                ins = op.fn(eobj)
                if op.chan is not None:
                    ins.then_inc(chans[op.chan], 16)
                elif op.sig:
                    ins.then_inc(sems[e], 1)
            if e == "sp":
                for c in final_chans:
                    if c in chans:
                        eobj.wait_ge(chans[c], cnt[c])

        block = es.enter_context(nc.Block())
        block.tensor(lambda t: run("pe", t))
        block.vector(lambda v: run("dve", v))
        block.scalar(lambda a: run("act", a))
        block.gpsimd(lambda g: run("pool", g))
        block.sync(lambda s: run("sp", s))


def build(NT=16, SEGB=16, STAGE=9):
    NBLK = NT * NB
    NTOK = NBLK * P
    nc = bass.Bass("TRN2", target_bir_lowering=False)
    es = contextlib.ExitStack()
    pg = Prog()

    def din(name, shape):
        return nc.dram_tensor(name, shape, F32, kind="ExternalInput").ap()

    x_d = din("x", [NTOK, D])
    w_in_d = din("w_in", [D, IN_W])
    w_out_d = din("w_out", [D, D])
    w_fi_d = din("w_ffn_in", [D, 2 * FH])
    w_fo_d = din("w_ffn_out", [FH, D])
    cst_d = din("cst", [P, C_END])
    prm_d = din("prm", [P, 32])
    gnb_d = din("gnb", [P, 1024])
    lnb_d = din("lnb", [P, 4 * D])
    y_d = nc.dram_tensor("y", [NTOK, D], F32, kind="ExternalOutput").ap()
    Ws = nc.dram_tensor("Ws", [G_TOT, P, 4096], BF16).ap()
    RetRec = nc.dram_tensor("RetRec", [NBLK, P, 4096], BF16).ap()
    AttRec = nc.dram_tensor("AttRec", [NBLK, P, 512], BF16).ap()

    def sb(name, shape, dt):
        return es.enter_context(nc.sbuf_tensor("sb_" + name, shape, dt))

    ring = [sb("ring%d" % i, [P, 4096], BF16) for i in range(3)]
    xbf = [sb("xbf%d" % i, [P, D], BF16) for i in range(2)]
    xT = sb("xT", [P, KC, T], BF16)
    x1buf = sb("x1buf", [P, NB, D], F32)
    arena = sb("arena", [P, FCH * T], BF16)
    attrec = sb("attrec", [P, 4, 512], BF16)
    retrec = [sb("retrec%d" % i, [P, 4, 1024], BF16) for i in range(2)]
    S = sb("S", [P, 1024], F32)
    Sfb = sb("Sfb", [P, 1024], BF16)
    ident = sb("ident", [P, P], BF16)
    ones = sb("ones", [P, P], BF16)
    biasT = sb("biasT", [P, 3, 8, P], BF16)
    Dcomb = sb("Dcomb", [P, 8, P], BF16)
    AFt = sb("AFt", [P, 8, P], BF16)
    ABt = sb("ABt", [P, 8, P], BF16)
    gnb = sb("gnb", [P, 1024], F32)
    lnb = sb("lnb", [P, 4, D], F32)
    tab = sb("tab", [P, 64], F32)
    prm = sb("prm", [P, 32], F32)
    esr = sb("esr", [P, 8, P], BF16)
    lbrow = sb("lbrow", [P, P], BF16)
    ones512 = sb("ones512", [1, 512], BF16)
    xb1 = xbf[0]
    ftmp = [sb("ftmp%d" % i, [P, 512], F32) for i in range(2)]
    st6 = sb("st6", [P, 8, 6], F32)
    mv = sb("mv", [P, 8, 2], F32)
    sd = sb("sd", [P, 8], F32)
    rstd = sb("rstd", [P, 8], F32)
    nb = sb("nb", [P, 8], F32)
    psb = [es.enter_context(nc.psum_tensor("ps%d" % i, [P, 512], F32)) for i in range(8)]

    hT = arena[:].rearrange("p (c t) -> p c t", c=FCH, t=T)
    QaT = arena[:, 0:8 * T].rearrange("p (c t) -> p c t", c=8, t=T)
    QrT = arena[:, 8 * T:16 * T].rearrange("p (c t) -> p c t", c=8, t=T)
    gsg = arena[:, 16 * T:24 * T].rearrange("p (b c) -> p b c", b=NB, c=1024)
    mixT = arena[:, 24 * T:40 * T].rearrange("p (c t) -> p c t", c=KC, t=T)
    krb = retrec[0]
    kdb = retrec[1]
    cvb = arena[:, 16384:20480]
    x1b16 = x1buf[:].rearrange("p b d -> p (b d)").bitcast(BF16)
    x1f = x1buf[:].rearrange("p b d -> p (b d)")
    rec1 = arena[:, 0:16384].rearrange("p (b q c) -> p b q c", b=NB, q=4, c=1024)
    stage = [x1f[:, 0:4096], x1f[:, 4096:8192]]
    def scr32(b, off, n):
        return x1f[:, b * D + off: b * D + off + n]

    def scr16(b, off32, n):
        return x1b16[:, (b * D + off32) * 2: (b * D + off32) * 2 + n]

    expT = [scr32(0, 0, 512), scr32(0, 512, 512)]
    PT = scr16(0, 1024, 1536).rearrange("p (k c) -> p k c", k=3, c=512)
    rden = scr32(1, 0, 512)
    PrT = scr16(1, 512, 1024).rearrange("p (h c) -> p h c", h=8, c=P)
    Qf = scr16(1, 1024, 1024).rearrange("p (h c) -> p h c", h=8, c=P)
    Qb = scr16(1, 1536, 1024).rearrange("p (h c) -> p h c", h=8, c=P)
    Qfs = [Qf, scr16(0, 0, 1024).rearrange("p (h c) -> p h c", h=8, c=P)]
    Qbs = [Qb, scr16(0, 512, 1024).rearrange("p (h c) -> p h c", h=8, c=P)]
    onorm = scr32(2, 0, 1024)
    retb = scr16(2, 1024, 1024)
    gtmp = [scr32(2, 1536, 256), scr32(2, 1792, 256)]
    SCR_BLOCK = {"expT0": 0, "expT1": 0, "PT": 0, "rden": 1, "PrT": 1, "Qf0": 1, "Qb0": 1, "Qf1": 0, "Qb1": 0, "onorm": 2, "retb": 2,
                 "gtmp0": 2, "gtmp1": 2}
    scr_seen = set()

    def scw(name):
        if name not in scr_seen:
            scr_seen.add(name)
            return [name, ("x1blk", SCR_BLOCK[name])]
        return [name]

    ps_rr = [0]

    def psalloc(n=1):
        r = []
        for _ in range(n):
            r.append(ps_rr[0] % 8)
            ps_rr[0] += 1
        return r

    def PSK(i):
        return ("ps", i)

    cp_rr = [0]

    def copy_any(out, in_, reads, writes, eng=None):
        if eng is None:
            eng = "dve" if cp_rr[0] % 2 == 0 else "act"
            cp_rr[0] += 1
        if eng == "dve":
            pg.add("dve", lambda e, o=out, i=in_: e.tensor_copy(o, i), reads, writes)
        elif eng == "act":
            pg.add("act", lambda e, o=out, i=in_: e.activation(out=o, in_=i, func=AF.Copy), reads, writes)
        else:
            pg.add("pool", lambda e, o=out, i=in_: e.tensor_copy(o, i), reads, writes)

    def mm(out, lhsT, rhs, start, stop, reads, writes):
        pg.add("pe", lambda e, o=out, l=lhsT, r=rhs, s=start, t=stop: e.matmul(o, l, r, start=s, stop=t), reads, writes)

    def tp(out, in_, reads, writes):
        pg.add("pe", lambda e, o=out, i=in_: e.transpose(o, i, ident[:]), reads + ["ident"], writes)

    def dma(q, out, in_, reads, writes, chan):
        pg.add(q, lambda e, o=out, i=in_: e.dma_start(out=o, in_=i), reads, writes, chan=chan)

    def act(out, in_, func, reads, writes, scale=None, bias=None):
        kw = {}
        if scale is not None:
            kw["scale"] = scale
        if bias is not None:
            kw["bias"] = bias
        pg.add("act", lambda e, o=out, i=in_, f=func, k=kw: e.activation(out=o, in_=i, func=f, **k), reads, writes)

    bar_t = sb("bar_t", [P, 8], F32)
    bar_n = [0]

    def barrier(keys):
        k = ("BAR", bar_n[0])
        bar_n[0] += 1
        pg.add("dve", lambda e: e.memset(bar_t[:], 0.0), [], list(keys) + [k])
        for eng in ["pe", "act", "pool", "sp"]:
            pg.add(eng, lambda e: e.nop(), [k], [])

    ring_n = [0]
    bg_hook = [None]

    def wload(g, ncols=4096):
        if bg_hook[0] is not None:
            bg_hook[0]()
        s = ring_n[0] % 3
        ring_n[0] += 1
        dma("sp", ring[s][:, 0:ncols], Ws[g, :, 0:ncols], [("Ws", g)], [("ring", s)], "ring%d" % s)
        return ring[s], ("ring", s)

    cstage = x1f[:, 0:C_END]
    dma("sp", cstage, cst_d[:, :], [], ["cstage"], "ld_c0")
    dma("sp", prm[:], prm_d[:, :], [], ["prm"], "ld_c1")
    dma("sp", gnb[:], gnb_d[:, :], [], ["gnb"], "ld_c2")
    dma("sp", lnb[:].rearrange("p a d -> p (a d)"), lnb_d[:, :], [], ["lnb"], "ld_c3")
    pg.add("dve", lambda e: e.memset(ones[:], 1.0), [], ["ones"])
    pg.add("dve", lambda e: e.memset(ones512[:], 1.0), [], ["ones"])
    pg.add("dve", lambda e: e.memset(S[:], 0.0), [], ["S"])
    pg.add("dve", lambda e: e.memset(Sfb[:], 0.0), [], ["Sfb"])
    pg.add("dve", lambda e: e.tensor_copy(ident[:], cstage[:, C_ID:C_ID + P]), ["cstage"], ["ident"])
    pg.add("dve", lambda e: e.tensor_copy(biasT[:].rearrange("p a b c -> p (a b c)"), cstage[:, C_BIAS:C_BIAS + 3072]),
           ["cstage"], ["biasT"])
    act(tab[:, 0:16], prm[:, 0:16], AF.Exp, ["prm"], ["tab"], scale=-1.0)
    act(tab[:, 0:16], tab[:, 0:16], AF.Ln, ["tab"], ["tab"], bias=1.0)
    pg.add("dve", lambda e: e.tensor_scalar(out=tab[:, 0:16], in0=tab[:, 0:16], scalar1=-1.0, scalar2=None, op0=ALU.mult),
           ["tab"], ["tab"])
    act(tab[:, TB_GCF:TB_GCF + 16], tab[:, 0:16], AF.Exp, ["tab"], ["tab"], scale=128.0)
    act(tab[:, TB_ES:TB_ES + 8], prm[:, 16:24], AF.Exp, ["prm", "tab"], ["tab"])
    pg.add("dve", lambda e: e.tensor_scalar(out=tab[:, TB_KDF:TB_KDF + 8], in0=tab[:, TB_LGF:TB_LGF + 8],
                                            scalar1=cstage[:, C_C127:C_C127 + 1], scalar2=LNS, op0=ALU.mult, op1=ALU.add),
           ["tab", "cstage"], ["tab"])
    pg.add("dve", lambda e: e.tensor_scalar(out=tab[:, TB_KDB:TB_KDB + 8], in0=tab[:, TB_LGB:TB_LGB + 8],
                                            scalar1=cstage[:, C_CJ:C_CJ + 1], scalar2=LNS, op0=ALU.mult, op1=ALU.add),
           ["tab", "cstage"], ["tab"])
    act(tab[:, TB_KDF:TB_KDF + 16], tab[:, TB_KDF:TB_KDF + 16], AF.Exp, ["tab"], ["tab"])
    pg.add("dve", lambda e: e.tensor_copy(tab[:, TB_LINK:TB_LINK + 1], prm[:, 24:25]), ["prm", "tab"], ["tab"])
    pg.add("dve", lambda e: e.tensor_scalar(out=tab[:, TB_LB:TB_LB + 1], in0=prm[:, 24:25], scalar1=-1.0, scalar2=-NEG,
                                            op0=ALU.add, op1=ALU.mult), ["prm", "tab"], ["tab"])
    tmpA = x1f[:, 4096:4096 + P]
    tmpB = x1f[:, 4096 + P:4096 + 2 * P]
    for h in range(8):
        pg.add("dve", lambda e, h=h: e.tensor_scalar(out=tmpA, in0=cstage[:, C_RP:C_RP + P], scalar1=tab[:, TB_LGF + h:TB_LGF + h + 1],
                                                     scalar2=None, op0=ALU.mult), ["cstage", "tab", "tmpB"], ["tmpA"])
        pg.add("dve", lambda e, h=h: e.scalar_tensor_tensor(out=tmpB, in0=cstage[:, C_RN:C_RN + P],
                                                            scalar=tab[:, TB_LGB + h:TB_LGB + h + 1], in1=tmpA,
                                                            op0=ALU.mult, op1=ALU.add), ["cstage", "tab", "tmpA"], ["tmpB"])
        act(Dcomb[:, h, :], tmpB, AF.Exp, ["tmpB"], ["Dcomb"], bias=None)
        act(AFt[:, h, :], cstage[:, C_IP1:C_IP1 + P], AF.Exp, ["cstage", "tab"], ["AFt"], scale=tab[:, TB_LGF + h:TB_LGF + h + 1])
        act(ABt[:, h, :], cstage[:, C_CMI:C_CMI + P], AF.Exp, ["cstage", "tab"], ["ABt"], scale=tab[:, TB_LGB + h:TB_LGB + h + 1])
        act(esr[:, h, :], ones[:], AF.Identity, ["ones", "tab"], ["esr"], scale=tab[:, TB_ES + h:TB_ES + h + 1])
    act(lbrow[:], ones[:], AF.Identity, ["ones", "tab"], ["lbrow"], scale=tab[:, TB_LB:TB_LB + 1])
    pg.add("dve", lambda e: e.tensor_scalar(out=Dcomb[:].rearrange("p a b -> p (a b)"), in0=Dcomb[:].rearrange("p a b -> p (a b)"),
                                            scalar1=SCALE, scalar2=None, op0=ALU.mult), ["Dcomb"], ["Dcomb"])
    barrier(["cstage", "tmpA", "tmpB", "stage0", "stage1", "tab", "Dcomb", "AFt", "ABt", "esr", "ident", "biasT", "ones"])

    w_in_v = w_in_d.rearrange("(kc p) c -> p kc c", p=P)
    w_out_v = w_out_d.rearrange("(kc p) c -> p kc c", p=P)
    w_fi_v = w_fi_d.rearrange("(kc p) c -> p kc c", p=P)
    w_fo_v = w_fo_d.rearrange("(kc p) c -> p kc c", p=P)
    FM_WIN = {0, 1, 2, 3, 4, 6, 7, 8, 9}

    def p0_load(g, seq):
        s = seq % 2
        st3 = stage[s].rearrange("p (kc c) -> p kc c", kc=KC, c=256)
        key = "stage%d" % s
        ch = "p0l%d" % s
        if g < G_WOUT:
            dma("sp", st3, w_in_v[:, :, g * 256:(g + 1) * 256], [], [key], ch)
        elif g < G_FIN:
            c0 = (g - G_WOUT) * 256
            dma("sp", st3, w_out_v[:, :, c0:c0 + 256], [], [key], ch)
        elif g < G_FOUT:
            c = g - G_FIN
            dma("sp", st3[:, :, 0:128], w_fi_v[:, :, c * 128:(c + 1) * 128], [], [key], ch)
            dma("sp", st3[:, :, 128:256], w_fi_v[:, :, FH + c * 128:FH + (c + 1) * 128], [], [key], ch)
        else:
            cg, piece = divmod(g - G_FOUT, 4)
            st11 = stage[s][:, 0:2816].rearrange("p (kc c) -> p kc c", kc=11, c=256)
            dma("sp", st11, w_fo_v[:, piece * 11:(piece + 1) * 11, cg * 256:(cg + 1) * 256], [], [key], ch)

    def p0_cast_store(g, seq, background=False):
        s = seq % 2
        r = seq % 3
        key = "stage%d" % s
        fm = (g in FM_WIN) or (G_FIN <= g < G_FOUT)
        if background:
            eng, dst, dkey, ch = "pool", cvb, "cvb", "p0sb"
        else:
            eng, dst, dkey, ch = ("dve" if seq % 2 == 0 else "act"), ring[r][:], ("ring", r), "p0s%d" % r
        if g >= G_FOUT:
            o = dst[:, 0:2816]
            i = stage[s][:, 0:2816]
            ncols = 2816
        elif fm:
            o = dst.rearrange("p (cc kc j) -> p kc cc j", cc=2, kc=KC, j=P)
            i = stage[s].rearrange("p (kc cc j) -> p kc cc j", kc=KC, cc=2, j=P)
            ncols = 4096
        else:
            o = dst
            i = stage[s]
            ncols = 4096
        copy_any(o, i, [key], [dkey], eng=eng)
        dma("pool", Ws[g, :, 0:ncols], dst[:, 0:ncols], [dkey], [("Ws", g)], ch)

    P0_FIRST = [4, 5] + list(range(10, 18))
    p0_rest = [g for g in range(G_TOT) if g not in P0_FIRST]
    if STAGE >= 2:
        for i in range(len(P0_FIRST) + 1):
            if i < len(P0_FIRST):
                p0_load(P0_FIRST[i], i)
            if i >= 1:
                p0_cast_store(P0_FIRST[i - 1], i - 1)

    def p0_gen(gs, base):
        for i in range(len(gs) + 1):
            if i < len(gs):
                p0_load(gs[i], base + i)
            if i >= 1:
                p0_cast_store(gs[i - 1], base + i - 1, background=True)
            yield

    bg = {"it": None, "per": 0.0, "acc": 0.0}

    def bg_tick():
        if bg["it"] is None:
            return
        bg["acc"] += bg["per"]
        while bg["acc"] >= 1.0:
            bg["acc"] -= 1.0
            try:
                next(bg["it"])
            except StopIteration:
                bg["it"] = None
                return

    barrier(["stage0", "stage1"] + [("ring", r) for r in range(3)] + [("Ws", g) for g in P0_FIRST])

    def load_xT(n, tb, slot):
        dma("pool", xbf[slot][:], x_d[n * P:(n + 1) * P, :], [], [("xbf", slot)], "xbf%d" % slot)
        for half in range(2):
            b = psalloc()[0]
            pv = psb[b][:].bitcast(BF16).rearrange("p (k c) -> p k c", k=8, c=P)
            for k in range(8):
                kc = half * 8 + k
                tp(pv[:, k, :], xbf[slot][:, kc * P:(kc + 1) * P], [("xbf", slot)], [PSK(b)])
            copy_any(xT[:, half * 8:(half + 1) * 8, tb * P:(tb + 1) * P], pv, [PSK(b)], [("xT", tb)])

    XT_ALL = [("xT", tb) for tb in range(NB)]

    def seg_last(n):
        return (n % SEGB == SEGB - 1) and n != NBLK - 1

    def seg_first(n):
        return (n % SEGB == 0) and n != 0

    xslot = [0]
    def p1_A(ti):
        n0 = ti * NB
        for tb in range(NB):
            load_xT(n0 + tb, tb, xslot[0] % 2)
            xslot[0] += 1
        wt, wk = wload(4)
        wv = wt[:].rearrange("p (cc kc j) -> p cc kc j", cc=2, kc=KC, j=P)
        for cc in range(2):
            b = psalloc()[0]
            for kc in range(KC):
                mm(psb[b][:], wv[:, cc, kc, :], xT[:, kc, :], kc == 0, kc == KC - 1, [wk] + XT_ALL, [PSK(b)])
            act(attrec[:, :, cc * P:(cc + 1) * P], psb[b][:].rearrange("p (b c) -> p b c", b=NB, c=P), AF.Copy, [PSK(b)], ["attasm"],
                scale=SCALE)
        wt, wk = wload(5)
        wv = wt[:].rearrange("p (kc c) -> p kc c", kc=KC, c=256)
        bs = psalloc(4)
        for tb in range(NB):
            o = psb[bs[tb]][:, 0:256]
            for kc in range(KC):
                mm(o, xT[:, kc, tb * P:(tb + 1) * P], wv[:, kc, :], kc == 0, kc == KC - 1, [wk, ("xT", tb)], [PSK(bs[tb])])
        for tb in range(NB):
            o = psb[bs[tb]][:, 0:256]
            copy_any(attrec[:, tb, 256:512], o, [PSK(bs[tb])], ["attasm"])
        dma("pool", AttRec[n0:n0 + NB].rearrange("n p c -> p n c"), attrec[:], ["attasm"], [("AttRec", ti)], "arst")

    def p1_B(ti):
        n0 = ti * NB
        for part in range(2):
            for cgi in range(4):
                wt, wk = wload(10 + part * 4 + cgi)
                wv = wt[:].rearrange("p (kc c) -> p kc c", kc=KC, c=256)
                bs = psalloc(4)
                for kc in range(KC):
                    for tb in range(NB):
                        o = psb[bs[tb]][:, 0:256]
                        mm(o, xT[:, kc, tb * P:(tb + 1) * P], wv[:, kc, :], kc == 0, kc == KC - 1, [wk, ("xT", tb)], [PSK(bs[tb])])
                for tb in range(NB):
                    bk = PSK(bs[tb])
                    o = psb[bs[tb]][:, 0:256]
                    cs = slice(cgi * 256, (cgi + 1) * 256)
                    if part == 1:
                        copy_any(rec1[:, tb, 2, cs], o, [bk], [("rec1v", tb, cgi)])
                    else:
                        copy_any(krb[:, tb, cs], o, [bk], [("krb", tb, cgi)])
                        for hh in range(2):
                            h = cgi * 2 + hh
                            hs = slice(h * P, (h + 1) * P)
                            oh = o[:, hh * P:(hh + 1) * P]
                            pg.add("dve", lambda e, tb=tb, hs=hs, oh=oh, h=h: e.tensor_scalar(
                                out=rec1[:, tb, 1, hs], in0=oh, scalar1=tab[:, TB_KDF + h:TB_KDF + h + 1], scalar2=None, op0=ALU.mult),
                                [bk, "tab"], [("rec1k", tb, cgi)])
                            pg.add("dve", lambda e, tb=tb, hs=hs, oh=oh, h=h: e.tensor_scalar(
                                out=kdb[:, tb, hs], in0=oh, scalar1=tab[:, TB_KDB + h:TB_KDB + h + 1], scalar2=None, op0=ALU.mult),
                                [bk, "tab"], [("kdb", tb, cgi)])
        for tb in range(NB):
            b = psalloc()[0]
            pv = psb[b][:].bitcast(BF16).rearrange("p (k c) -> p k c", k=8, c=P)
            for h in range(8):
                tp(pv[:, h, :], krb[:, tb, h * P:(h + 1) * P], [("krb", tb, h // 2)], [PSK(b)])
            copy_any(rec1[:, tb, 0, :].rearrange("p (k c) -> p k c", k=8, c=P), pv, [PSK(b)], [("rec1t", tb)])

    def p1_C(ti):
        n0 = ti * NB
        for tb in range(NB - 1, -1, -1):
            n = n0 + tb
            if seg_last(n):
                pg.add("dve", lambda e: e.tensor_scalar(out=S[:], in0=S[:], scalar1=tab[:, TB_LINK:TB_LINK + 1], scalar2=None,
                                                        op0=ALU.mult), ["S", "tab"], ["S"])
            copy_any(rec1[:, tb, 3, :], S[:], ["S"], [("rec1s", tb)], eng="act")
            bs = psalloc(2)
            for h in range(8):
                o = psb[bs[h // 4]][:, (h % 4) * P:(h % 4 + 1) * P]
                mm(o, kdb[:, tb, h * P:(h + 1) * P], rec1[:, tb, 2, h * P:(h + 1) * P], True, True,
                   [("kdb", tb, h // 2), ("rec1v", tb, h // 2)], [PSK(bs[h // 4])])
            for h in range(8):
                o = psb[bs[h // 4]][:, (h % 4) * P:(h % 4 + 1) * P]
                pg.add("dve", lambda e, h=h, o=o: e.scalar_tensor_tensor(
                    out=S[:, h * P:(h + 1) * P], in0=S[:, h * P:(h + 1) * P], scalar=tab[:, TB_GCB + h:TB_GCB + h + 1], in1=o,
                    op0=ALU.mult, op1=ALU.add), ["S", "tab", PSK(bs[h // 4])], ["S"])
            rk = [("rec1t", tb), ("rec1s", tb)] + [("rec1k", tb, c) for c in range(4)] + [("rec1v", tb, c) for c in range(4)]
            dma("pool", RetRec[n], rec1[:, tb].rearrange("p q c -> p (q c)"), rk, [("RetRec", n)], "rrst%d" % tb)

    if STAGE >= 3:
        bg["it"] = p0_gen(p0_rest, len(P0_FIRST))
        bg["per"] = (len(p0_rest) + 1) / float(NT * 10 - 4)
        bg_hook[0] = bg_tick
        p1_A(NT - 1)
        p1_B(NT - 1)
        for ti in range(NT - 2, -1, -1):
            p1_A(ti)
            p1_C(ti + 1)
            p1_B(ti)
        p1_C(0)
    bg_hook[0] = None
    if bg["it"] is not None:
        for _ in bg["it"]:
            pass
        bg["it"] = None
    bk = ["S", "attasm"] + [("RetRec", n) for n in range(NBLK)] + [("AttRec", t) for t in range(NT)]
    for tb in range(NB):
        bk += [("rec1t", tb), ("rec1s", tb)] + [("rec1k", tb, c) for c in range(4)] + [("rec1v", tb, c) for c in range(4)]
        bk += [("krb", tb, c) for c in range(4)] + [("kdb", tb, c) for c in range(4)] + [("xT", tb)]
    barrier(bk + [("ring", r) for r in range(3)] + ["stage0", "stage1", "cvb"] + [("Ws", g) for g in range(G_TOT)])
    pg.add("dve", lambda e: e.memset(S[:], 0.0), ["S"], ["S"])

    lst6 = sb("lst6", [P, NB, 8, 6], F32)
    lmv = sb("lmv", [P, NB, 2], F32)
    lsd = sb("lsd", [P, NB], F32)
    lrs = sb("lrs", [P, NB], F32)
    lnbb = sb("lnbb", [P, NB], F32)
    PTs = [PT, scr16(3, 0, 1536).rearrange("p (k c) -> p k c", k=3, c=512)]
    rdens = [rden, scr32(3, 768, 512)]
    SCR_BLOCK.update({"PT0": 0, "PT1": 3, "rden0": 1, "rden1": 3})

    def evac_resid(bs, cgi):
        for tb in range(NB):
            o = psb[bs[tb]][:, 0:256]
            xs = x1buf[:, tb, cgi * 256:(cgi + 1) * 256]
            pg.add("dve", lambda e, o=o, xs=xs: e.scalar_tensor_tensor(out=xs, in0=xs, scalar=ALPHA, in1=o, op0=ALU.mult, op1=ALU.add),
                   [PSK(bs[tb]), ("x1", tb, cgi)], [("x1", tb, cgi)])
            pg.add("dve", lambda e, tb=tb, cgi=cgi, xs=xs: e.bn_stats(out=lst6[:, tb, cgi, :], in_=xs), [("x1", tb, cgi)], [("lst6", tb, cgi)])

    def ln_steps(which, after):
        steps = []

        def s0():
            for tb in range(NB):
                pg.add("dve", lambda e, tb=tb: e.bn_aggr(out=lmv[:, tb, :], in_=lst6[:, tb].rearrange("p a b -> p (a b)")),
                       [("lst6", tb, c) for c in range(8)], [("lmv", tb)])
            lk = [("lmv", tb) for tb in range(NB)]
            act(lsd[:], lmv[:, :, 1], AF.Ln, lk, ["lsd"], bias=EPS)
            act(lrs[:], lsd[:], AF.Exp, ["lsd"], ["lrs"], scale=-0.5)
        steps.append(s0)
        for tb in range(NB):
            def st(tb=tb):
                xkeys = [("x1", tb, c) for c in range(8)]
                xv = x1buf[:, tb, :]
                pg.add("dve", lambda e, xv=xv, tb=tb: e.scalar_tensor_tensor(
                    out=xv, in0=xv, scalar=lmv[:, tb, 0:1], in1=lnb[:, 2 * which, :], op0=ALU.subtract, op1=ALU.mult),
                    xkeys + ["lnb", ("lmv", tb)], xkeys)
                pg.add("dve", lambda e, xv=xv, tb=tb: e.scalar_tensor_tensor(
                    out=xv, in0=xv, scalar=lrs[:, tb:tb + 1], in1=lnb[:, 2 * which + 1, :], op0=ALU.mult, op1=ALU.add),
                    xkeys + ["lnb", "lrs"], xkeys)
                after(tb, xkeys)
            steps.append(st)
        return steps

    def ln_finish(which, after):
        for s in ln_steps(which, after):
            s()

    def front1(ti):
        n0 = ti * NB
        for tb in range(NB):
            load_xT(n0 + tb, tb, xslot[0] % 2)
            xslot[0] += 1
        proj_q(0, QaT, "QaT")

    def proj_q(g0, dst, nm, steps=None):
        for gi in range(4):
            wt, wk = wload(g0 + gi)
            wv = wt[:].rearrange("p (cc kc j) -> p cc kc j", cc=2, kc=KC, j=P)
            for cc in range(2):
                c = gi * 2 + cc
                b = psalloc()[0]
                for kc in range(KC):
                    mm(psb[b][:], wv[:, cc, kc, :], xT[:, kc, :], kc == 0, kc == KC - 1, [wk] + XT_ALL, [PSK(b)])
                copy_any(dst[:, c, :], psb[b][:], [PSK(b)], [(nm, c)])
                if steps:
                    steps.pop(0)()

    def front2(ti, steps=None):
        scr_seen.clear()
        proj_q(6, QrT, "QrT", steps)
        while steps:
            steps.pop(0)()
        for cgi in range(4):
            wt, wk = wload(18 + cgi)
            wv = wt[:].rearrange("p (kc c) -> p kc c", kc=KC, c=256)
            bs = psalloc(4)
            for kc in range(KC):
                for tb in range(NB):
                    mm(psb[bs[tb]][:, 0:256], xT[:, kc, tb * P:(tb + 1) * P], wv[:, kc, :], kc == 0, kc == KC - 1,
                       [wk, ("xT", tb)], [PSK(bs[tb])])
            for tb in range(NB):
                gi_ = tb % 2
                gname = "gtmp%d" % gi_
                act(gtmp[gi_], psb[bs[tb]][:, 0:256], AF.Silu, [PSK(bs[tb])], scw(gname))
                pg.add("pool", lambda e, tb=tb, cgi=cgi, gi_=gi_: e.tensor_tensor(
                    out=gsg[:, tb, cgi * 256:(cgi + 1) * 256], in0=gtmp[gi_], in1=gnb[:, cgi * 256:(cgi + 1) * 256], op=ALU.mult),
                    [gname, "gnb"], [("gsg", tb)])

    cyc_n = [0]

    def cyc():
        cyc_n[0] += 1
        return cyc_n[0] % 2

    def blk_ctx(ti, tb):
        n = ti * NB + tb
        return {"n": n, "tb": tb, "ts": slice(tb * P, (tb + 1) * P), "rs": n % 2,
                "kbs": [kb for kb in range(3) if 0 <= n + kb - 1 < NBLK]}

    def blk_E1(c):
        n, tb, ts_, rs, kbs = c["n"], c["tb"], c["ts"], c["rs"], c["kbs"]
        if n == 0:
            dma("pool", attrec[:, 0, :], AttRec[0], [("AttRec", 0)], [("attrec", 0)], "arld0")
        if n + 1 < NBLK:
            a_ = (n + 1) % 4
            dma("pool", attrec[:, a_, :], AttRec[n + 1], [("AttRec", (n + 1) // NB)], [("attrec", a_)], "arld%d" % a_)
        rk = ("retrec", rs)
        rec = retrec[rs]
        qrk = [("QrT", q) for q in range(8)]
        qi = n % 2
        pg.add("pool", lambda e, ts_=ts_, qi=qi: e.tensor_tensor(out=Qfs[qi], in0=QrT[:, :, ts_], in1=AFt[:], op=ALU.mult),
               qrk + ["AFt"], scw("Qf%d" % qi))
        pg.add("pool", lambda e, ts_=ts_, qi=qi: e.tensor_tensor(out=Qbs[qi], in0=QrT[:, :, ts_], in1=ABt[:], op=ALU.mult),
               qrk + ["ABt"], scw("Qb%d" % qi))
        for q in range(2):
            b = cyc()
            for hh in range(4):
                h = q * 4 + hh
                mm(psb[b][:, hh * P:(hh + 1) * P], rec[:, 0, h * P:(h + 1) * P], QrT[:, h, ts_], True, True, [rk, ("QrT", h)], [PSK(b)])
            pg.add("dve", lambda e, q=q, b=b: e.tensor_tensor(
                out=PrT[:, q * 4:(q + 1) * 4, :], in0=psb[b][:].rearrange("p (g c) -> p g c", g=4, c=P),
                in1=Dcomb[:, q * 4:(q + 1) * 4, :], op=ALU.mult), [PSK(b), "Dcomb"], scw("PrT"))
        for kv in range(2):
            hs4 = slice(kv * 4, (kv + 1) * 4)
            qk = [("QaT", q) for q in range(kv * 4, kv * 4 + 4)]
            ptn = "PT%d" % kv
            for ki, kb in enumerate(kbs):
                a_ = (n + kb - 1) % 4
                b = cyc()
                cross = (kb == 0 and seg_first(n)) or (kb == 2 and seg_last(n))
                mm(psb[b][:].rearrange("p (g c) -> p g c", g=4, c=P), attrec[:, a_, kv * P:(kv + 1) * P], QaT[:, hs4, ts_],
                   True, False, [("attrec", a_)] + qk, [PSK(b)])
                mm(psb[b][:], ident[:], biasT[:, kb, hs4, :].rearrange("p a b -> p (a b)"), False, not cross, ["ident", "biasT"], [PSK(b)])
                if cross:
                    mm(psb[b][:], lbrow[0:1, :], ones512[0:1, :], False, True,
                       ["lbrow", "ones"], [PSK(b)])
                act(PTs[kv][:, ki, :], psb[b][:], AF.Exp, [PSK(b)], scw(ptn))

    def blk_E2(c):
        n, tb, ts_, rs, kbs = c["n"], c["tb"], c["ts"], c["rs"], c["kbs"]
        rk = ("retrec", rs)
        rec = retrec[rs]
        for kv in range(2):
            hs4 = slice(kv * 4, (kv + 1) * 4)
            ptn = "PT%d" % kv
            rn = "rden%d" % kv
            bo = 2 + kv
            bd = cyc()
            for ki, kb in enumerate(kbs):
                a_ = (n + kb - 1) % 4
                mm(psb[bo][:], attrec[:, a_, 256 + kv * P:256 + (kv + 1) * P], PTs[kv][:, ki, :], ki == 0, ki == len(kbs) - 1,
                   [("attrec", a_), ptn], [PSK(bo)])
            for ki, kb in enumerate(kbs):
                mm(psb[bd][:], ones[:], PTs[kv][:, ki, :], ki == 0, False, ["ones", ptn], [PSK(bd)])
            mm(psb[bd][:], ones[0:1, :], esr[0:1, hs4, :].rearrange("p a b -> p (a b)"), False, True, ["ones", "esr"], [PSK(bd)])
            act(rdens[kv], psb[bd][:], AF.Ln, [PSK(bd)], scw(rn))
            act(rdens[kv], rdens[kv], AF.Exp, [rn], [rn], scale=-1.0)
        rb = 4 + 2 * (n % 2)
        c["rb"] = rb
        for h in range(8):
            bo = rb + h // 4
            o = psb[bo][:, (h % 4) * P:(h % 4 + 1) * P]
            hs = slice(h * P, (h + 1) * P)
            mm(o, PrT[:, h, :], rec[:, 2, hs], True, False, ["PrT", rk], [PSK(bo)])
            mm(o, Qfs[n % 2][:, h, :], Sfb[:, hs], False, False, ["Qf%d" % (n % 2), "Sfb"], [PSK(bo)])
            mm(o, Qbs[n % 2][:, h, :], rec[:, 3, hs], False, True, ["Qb%d" % (n % 2), rk], [PSK(bo)])
        sbk = [cyc(), cyc()]
        for h in range(8):
            bo = sbk[h // 4]
            o = psb[bo][:, (h % 4) * P:(h % 4 + 1) * P]
            hs = slice(h * P, (h + 1) * P)
            mm(o, rec[:, 1, hs], rec[:, 2, hs], True, True, [rk], [PSK(bo)])
        for h in range(8):
            o = psb[sbk[h // 4]][:, (h % 4) * P:(h % 4 + 1) * P]
            pg.add("dve", lambda e, h=h, o=o: e.scalar_tensor_tensor(
                out=S[:, h * P:(h + 1) * P], in0=S[:, h * P:(h + 1) * P], scalar=tab[:, TB_GCF + h:TB_GCF + h + 1], in1=o,
                op0=ALU.mult, op1=ALU.add), ["S", "tab", PSK(sbk[h // 4])], ["S"])
        if seg_last(n):
            pg.add("dve", lambda e: e.tensor_scalar(out=S[:], in0=S[:], scalar1=tab[:, TB_LINK:TB_LINK + 1], scalar2=None,
                                                    op0=ALU.mult), ["S", "tab"], ["S"])
        copy_any(Sfb[:], S[:], ["S"], ["Sfb"], eng="act")

    def blk_late(c):
        n, tb, ts_ = c["n"], c["tb"], c["ts"]
        for kv in range(2):
            hs4 = slice(kv * 4, (kv + 1) * 4)
            bo = 2 + kv
            pg.add("dve", lambda e, bo=bo, hs4=hs4, kv=kv: e.tensor_tensor(
                out=mixT[:, hs4, ts_], in0=psb[bo][:].rearrange("p (g c) -> p g c", g=4, c=P),
                in1=rdens[kv].rearrange("p (g c) -> p g c", g=4, c=P), op=ALU.mult), [PSK(bo), "rden%d" % kv], [("mixT", kv, tb)])
        rb = c["rb"]
        for h in range(8):
            o = psb[rb + h // 4][:, (h % 4) * P:(h % 4 + 1) * P]
            pg.add("dve", lambda e, h=h, o=o: e.bn_stats(out=st6[:, h, :], in_=o), [PSK(rb + h // 4)], ["st6"])
        for h in range(8):
            pg.add("dve", lambda e, h=h: e.bn_aggr(out=mv[:, h, :], in_=st6[:, h, :]), ["st6"], ["mv"])
        act(sd[:], mv[:, :, 1], AF.Ln, ["mv"], ["sd"], bias=EPS)
        act(rstd[:], sd[:], AF.Exp, ["sd"], ["rstd"], scale=-0.5)
        for h in range(8):
            o = psb[rb + h // 4][:, (h % 4) * P:(h % 4 + 1) * P]
            pg.add("dve", lambda e, h=h, o=o: e.tensor_scalar(
                out=onorm[:, h * P:(h + 1) * P], in0=o, scalar1=mv[:, h, 0:1], scalar2=rstd[:, h:h + 1],
                op0=ALU.subtract, op1=ALU.mult), [PSK(rb + h // 4), "rstd", "mv"], scw("onorm"))
        pg.add("pool", lambda e: e.tensor_tensor(out=retb, in0=onorm, in1=gsg[:, tb, :], op=ALU.mult),
               ["onorm", ("gsg", tb)], scw("retb"))

    def blk_lateB(c):
        n, tb, ts_ = c["n"], c["tb"], c["ts"]
        b = cyc()
        pv = psb[b][:].bitcast(BF16).rearrange("p (k c) -> p k c", k=8, c=P)
        for h in range(8):
            tp(pv[:, h, :], retb[:, h * P:(h + 1) * P], ["retb"], [PSK(b)])
        copy_any(mixT[:, 8:16, ts_], pv, [PSK(b)], [("mixT", 2, tb)], eng="act")

    def rec_load(c):
        n, rs = c["n"], c["rs"]
        dma("pool", retrec[rs][:].rearrange("p q c -> p (q c)"), RetRec[n], [("RetRec", n)], [("retrec", rs)], "rrld%d" % rs)

    def attn_ret_tile(ti):
        cs = [blk_ctx(ti, tb) for tb in range(NB)]
        rec_load(cs[0])
        rec_load(cs[1])
        blk_E1(cs[0])
        blk_E2(cs[0])
        rec_load(cs[2])
        for tb in range(1, NB):
            blk_E1(cs[tb])
            blk_late(cs[tb - 1])
            blk_E2(cs[tb])
            if tb + 2 < NB:
                rec_load(cs[tb + 2])
            blk_lateB(cs[tb - 1])
        blk_late(cs[NB - 1])
        blk_lateB(cs[NB - 1])

    def mid(ti):
        n0 = ti * NB
        attn_ret_tile(ti)
        allscr = list(SCR_BLOCK.keys())
        for tb in range(NB):
            wk_ = [("x1", tb, c) for c in range(8)] + [("x1blk", tb)] + [k for k in allscr if SCR_BLOCK[k] == tb]
            dma("pool", x1buf[:, tb, :], x_d[(n0 + tb) * P:(n0 + tb + 1) * P, :], [], wk_, "x1ld%d" % tb)
        for cgi in range(8):
            wt, wk = wload(G_WOUT + cgi)
            wv = wt[:].rearrange("p (kc c) -> p kc c", kc=KC, c=256)
            bs = psalloc(4)
            for kc in range(KC):
                grp = 0 if kc < 4 else (1 if kc < 8 else 2)
                for tb in range(NB):
                    mm(psb[bs[tb]][:, 0:256], mixT[:, kc, tb * P:(tb + 1) * P], wv[:, kc, :], kc == 0, kc == KC - 1,
                       [wk, ("mixT", grp, tb)], [PSK(bs[tb])])
            evac_resid(bs, cgi)

        def after1(tb, xkeys):
            copy_any(xb1[:], x1buf[:, tb, :], xkeys, [("xbf", 0)], eng="act")
            for half in range(2):
                b = psalloc()[0]
                pv = psb[b][:].bitcast(BF16).rearrange("p (k c) -> p k c", k=8, c=P)
                for k in range(8):
                    kc = half * 8 + k
                    tp(pv[:, k, :], xb1[:, kc * P:(kc + 1) * P], [("xbf", 0)], [PSK(b)])
                copy_any(xT[:, half * 8:(half + 1) * 8, tb * P:(tb + 1) * P], pv, [PSK(b)], [("xT", tb)], eng="act")
        ln_finish(0, after1)
        for c in range(FCH):
            wt, wk = wload(G_FIN + c)
            wv = wt[:].rearrange("p (cc kc j) -> p cc kc j", cc=2, kc=KC, j=P)
            bg, bu = psalloc(2)
            for kc in range(KC):
                mm(psb[bg][:], wv[:, 0, kc, :], xT[:, kc, :], kc == 0, kc == KC - 1, [wk] + XT_ALL, [PSK(bg)])
            for kc in range(KC):
                mm(psb[bu][:], wv[:, 1, kc, :], xT[:, kc, :], kc == 0, kc == KC - 1, [wk] + XT_ALL, [PSK(bu)])
            fi = c % 2
            act(ftmp[fi][:], psb[bg][:], AF.Silu, [PSK(bg)], [("ftmp", fi)])
            pg.add("dve", lambda e, c=c, fi=fi, bu=bu: e.tensor_tensor(out=hT[:, c, :], in0=ftmp[fi][:], in1=psb[bu][:], op=ALU.mult),
                   [("ftmp", fi), PSK(bu)], [("hT", c)])
        for cgi in range(8):
            bs = psalloc(4)
            for piece in range(4):
                wt, wk = wload(G_FOUT + cgi * 4 + piece, 2816)
                wv = wt[:, 0:2816].rearrange("p (kc c) -> p kc c", kc=11, c=256)
                for kk in range(11):
                    c = piece * 11 + kk
                    for tb in range(NB):
                        mm(psb[bs[tb]][:, 0:256], hT[:, c, tb * P:(tb + 1) * P], wv[:, kk, :], c == 0, c == FCH - 1,
                           [wk, ("hT", c)], [PSK(bs[tb])])
            evac_resid(bs, cgi)

    def ln2_store(ti):
        n0 = ti * NB

        def after2(tb, xkeys):
            dma("pool", y_d[(n0 + tb) * P:(n0 + tb + 1) * P, :], x1buf[:, tb, :], xkeys + [("x1blk", tb)], [("y", n0 + tb)], "yst%d" % tb)
        return ln_steps(1, after2)

    if STAGE >= 4:
        front1(0)
        pend = None
        for ti in range(NT):
            front2(ti, pend)
            mid(ti)
            if ti + 1 < NT:
                front1(ti + 1)
                pend = ln2_store(ti)
            else:
                for s in ln2_store(ti):
                    s()


    pg.emit(nc, es, ["yst%d" % tb for tb in range(NB)])
    es.close()
    return nc


def make_consts():
    c = np.zeros((P, C_END), np.float32)
    j = np.arange(P)[:, None].astype(np.float32)
    i = np.arange(P)[None, :].astype(np.float32)
    c[:, C_RP:C_RP + P] = np.maximum(i - j, 0)
    c[:, C_RN:C_RN + P] = np.maximum(j - i, 0)
    c[:, C_IP1:C_IP1 + P] = i + 1
    c[:, C_CMI:C_CMI + P] = 128 - i
    c[:, C_C127] = 127 - np.arange(P)
    c[:, C_CJ] = np.arange(P)
    bias = np.zeros((P, 3, 8, P), np.float32)
    for kb in range(3):
        kpos = (kb - 1) * P + j
        dist = np.abs(i - kpos)
        for h in range(8):
            slope = 2.0 ** (-(h + 1))
            bias[:, kb, h, :] = np.where(dist <= P, -slope * dist, NEG)
    c[:, C_BIAS:C_BIAS + 3072] = bias.reshape(P, 3072)
    c[:, C_ID:C_ID + P] = np.eye(P, dtype=np.float32)
    return c


def _bc(v, n):
    return np.ascontiguousarray(np.broadcast_to(np.asarray(v, np.float32).reshape(1, -1), (P, n)))


_NC_CACHE = {}


def kernel(x_prompt, x_sample, w_in, attn_sink, ret_decay_fwd, ret_decay_bwd, ret_gn_gain, w_out,
           ln1_gain, ln1_bias, w_ffn_in, w_ffn_out, ln2_gain, ln2_bias):
    x_prompt = np.asarray(x_prompt, np.float32)
    x_sample = np.asarray(x_sample, np.float32)
    if "nc" not in _NC_CACHE:
        _NC_CACHE["nc"] = build(16, 16)
    nc = _NC_CACHE["nc"]
    cst = make_consts()
    gnb = _bc(ret_gn_gain, 1024)
    lnb = np.concatenate([_bc(ln1_gain, D), _bc(ln1_bias, D), _bc(ln2_gain, D), _bc(ln2_bias, D)], axis=1)
    shared = {
        "w_in": np.ascontiguousarray(np.asarray(w_in, np.float32)[0]),
        "w_out": np.ascontiguousarray(np.asarray(w_out, np.float32)[0]),
        "w_ffn_in": np.ascontiguousarray(np.asarray(w_ffn_in, np.float32)[0]),
        "w_ffn_out": np.ascontiguousarray(np.asarray(w_ffn_out, np.float32)[0]),
        "cst": cst, "gnb": gnb, "lnb": np.ascontiguousarray(lnb),
    }
    in_maps = []
    for core in range(8):
        prm = np.zeros((P, 32), np.float32)
        prm[:, 0:8] = _bc(ret_decay_fwd, 8)
        prm[:, 8:16] = _bc(ret_decay_bwd, 8)
        prm[:, 16:24] = _bc(attn_sink, 8)
        if core < 4:
            xc = x_prompt[core]
            prm[:, 24] = 1.0
        else:
            xc = x_sample[(core - 4) * 4:(core - 3) * 4].reshape(8192, D)
            prm[:, 24] = 0.0
        m = dict(shared)
        m["x"] = np.ascontiguousarray(xc)
        m["prm"] = prm
        in_maps.append(m)
    res = run_bass_kernel_spmd(nc, in_maps, core_ids=list(range(8)))
    ys = [np.asarray(r["y"], np.float32) for r in res.results]
    y_prompt = np.stack(ys[0:4], axis=0)
    y_sample = np.concatenate([y.reshape(4, 2048, D) for y in ys[4:8]], axis=0)
    return (y_prompt, y_sample)
```
